# Optimizing a Trainium2 kernel written in Bass

```python
import jax, jax.numpy as jnp
from jax import lax
import numpy as np

D_MODEL = 2048
BATCH = 8
SEQ = 2048
DEPTH = 2

N_META = 16
MLSTM_W = D_MODEL // 2
CONV_W = D_MODEL - MLSTM_W
MLSTM_HEADS = 4
DV = MLSTM_W // MLSTM_HEADS
DQK = DV // 2
QK_W = MLSTM_HEADS * DQK
CHUNK = 64
CONV_K = 3
D_FF = -(-8 * D_MODEL // (3 * 256)) * 256
GATE_CAP = 15.0
EPS = 1e-6
SPLIT_SIZES = (QK_W, QK_W, MLSTM_W, MLSTM_W, MLSTM_HEADS, MLSTM_HEADS, CONV_W, CONV_W, CONV_W)
D_IN = sum(SPLIT_SIZES)

kernel_name = "hymba_mlstm_shortconv_swiglu"


def rmsnorm(x, w):
    xf = x.astype(jnp.float32)
    y = xf * lax.rsqrt(jnp.mean(xf * xf, axis=-1, keepdims=True) + EPS)
    return (y * w.astype(jnp.float32)).astype(x.dtype)


def mlstm_chunkwise(q, k, v, log_i, log_f):
    b_, h_, t_, _ = q.shape
    nc = t_ // CHUNK

    def to_chunks(a):
        return jnp.moveaxis(a.reshape(a.shape[:2] + (nc, CHUNK) + a.shape[3:]), 2, 0)

    causal = jnp.tril(jnp.ones((CHUNK, CHUNK), dtype=bool))

    def step(carry, inp):
        c_st, n_st, m_st = carry
        qb, kb, vb, li, lf = inp
        b = jnp.cumsum(lf, axis=-1)
        dmat = jnp.where(causal, b[..., :, None] - b[..., None, :] + li[..., None, :], -jnp.inf)
        inter = b + m_st[..., None]
        m_t = jnp.maximum(inter, jnp.max(dmat, axis=-1))
        w_inter = jnp.exp(inter - m_t)
        s_w = jnp.einsum('bhtd,bhsd->bhts', qb, kb) * jnp.exp(dmat - m_t[..., None])
        num = (w_inter[..., None] * jnp.einsum('bhtd,bhde->bhte', qb, c_st)
               + jnp.einsum('bhts,bhse->bhte', s_w, vb))
        den = w_inter * jnp.einsum('bhtd,bhd->bht', qb, n_st) + jnp.sum(s_w, axis=-1)
        h = num / jnp.maximum(jnp.abs(den), jnp.exp(-m_t))[..., None]
        b_end = b[..., -1]
        decay = b_end[..., None] - b + li
        m_new = jnp.maximum(b_end + m_st, jnp.max(decay, axis=-1))
        w_old = jnp.exp(b_end + m_st - m_new)
        w_in = jnp.exp(decay - m_new[..., None])
        c_new = w_old[..., None, None] * c_st + jnp.einsum('bhs,bhsd,bhse->bhde', w_in, kb, vb)
        n_new = w_old[..., None] * n_st + jnp.einsum('bhs,bhsd->bhd', w_in, kb)
        return (c_new, n_new, m_new), h

    init = (jnp.zeros((b_, h_, q.shape[-1], v.shape[-1]), jnp.float32),
            jnp.zeros((b_, h_, q.shape[-1]), jnp.float32),
            jnp.zeros((b_, h_), jnp.float32))
    _, hs = lax.scan(step, init, tuple(map(to_chunks, (q, k, v, log_i, log_f))))
    return jnp.moveaxis(hs, 0, 2).reshape(b_, h_, t_, v.shape[-1])


def mlstm_group(q, k, v, i_raw, f_raw):
    seq_len = q.shape[1]
    pad_front = (-N_META) % CHUNK
    pad_back = (-(pad_front + seq_len)) % CHUNK
    tr = lambda a: jnp.moveaxis(a.astype(jnp.float32), 1, 2)
    q, k, v = tr(q) * (DQK ** -0.5), tr(k), tr(v)
    log_i = tr(GATE_CAP * jnp.tanh(i_raw.astype(jnp.float32) / GATE_CAP))
    log_f = jax.nn.log_sigmoid(tr(GATE_CAP * jnp.tanh(f_raw.astype(jnp.float32) / GATE_CAP)))
    pad4 = ((0, 0), (0, 0), (pad_front, pad_back), (0, 0))
    pad3 = ((0, 0), (0, 0), (pad_front, pad_back))
    q, k, v = jnp.pad(q, pad4), jnp.pad(k, pad4), jnp.pad(v, pad4)
    log_i = jnp.pad(log_i, pad3, constant_values=-jnp.inf)
    log_f = jnp.pad(log_f, pad3)
    h = mlstm_chunkwise(q, k, v, log_i, log_f)
    return h[:, :, pad_front:pad_front + seq_len]


def short_conv_group(u, gate_b, gate_c, conv_w):
    a = gate_c * u
    seq_len = a.shape[1]
    ap = jnp.pad(a, ((0, 0), (CONV_K - 1, 0), (0, 0)))
    conv = sum(ap[:, j:j + seq_len] * conv_w[j] for j in range(CONV_K))
    return gate_b * conv


def setup_inputs(seed: int = 0) -> dict:
    key = jax.random.key(seed)
    ks = jax.random.split(key, 14)
    nrm = lambda k, shape, s: jax.random.normal(k, shape, jnp.float32) * s
    gain = lambda k, shape: 1.0 + 0.02 * jax.random.normal(k, shape, jnp.float32)
    b_i = nrm(ks[4], (DEPTH, MLSTM_HEADS), 0.1)
    b_f = 3.0 + nrm(ks[5], (DEPTH, MLSTM_HEADS), 0.5)
    return {
        "x": nrm(ks[0], (BATCH, SEQ, D_MODEL), 1.0),
        "meta_tokens": nrm(ks[1], (N_META, D_MODEL), 1.0),
        "norm_mix_w": gain(ks[2], (DEPTH, D_MODEL)),
        "w_in": nrm(ks[3], (DEPTH, D_MODEL, D_IN), D_MODEL ** -0.5),
        "b_gates": jnp.concatenate([b_i, b_f], axis=-1),
        "conv_w": nrm(ks[6], (DEPTH, CONV_K, CONV_W), CONV_K ** -0.5),
        "mlstm_norm_w": gain(ks[7], (DEPTH, MLSTM_W)),
        "w_out": nrm(ks[8], (DEPTH, D_MODEL, D_MODEL), D_MODEL ** -0.5),
        "norm_ffn_w": gain(ks[9], (DEPTH, D_MODEL)),
        "w_gate": nrm(ks[10], (DEPTH, D_MODEL, D_FF), D_MODEL ** -0.5),
        "w_up": nrm(ks[11], (DEPTH, D_MODEL, D_FF), D_MODEL ** -0.5),
        "w_down": nrm(ks[12], (DEPTH, D_FF, D_MODEL), D_FF ** -0.5),
        "norm_final_w": gain(ks[13], (D_MODEL,)),
    }


def reference(x, meta_tokens, norm_mix_w, w_in, b_gates, conv_w, mlstm_norm_w, w_out,
              norm_ffn_w, w_gate, w_up, w_down, norm_final_w):
    bsz = x.shape[0]
    meta = jnp.broadcast_to(meta_tokens.astype(x.dtype)[None], (bsz, N_META, D_MODEL))
    h = jnp.concatenate([meta, x], axis=1)
    seq_len = h.shape[1]
    split_points = np.cumsum(SPLIT_SIZES)[:-1].tolist()
    for l in range(DEPTH):
        hn = rmsnorm(h, norm_mix_w[l])
        proj = hn @ w_in[l]
        q, k, v, og, ig, fg, u, gb, gc = jnp.split(proj, split_points, axis=-1)
        ig = ig + b_gates[l, :MLSTM_HEADS]
        fg = fg + b_gates[l, MLSTM_HEADS:]
        hm = mlstm_group(q.reshape(bsz, seq_len, MLSTM_HEADS, DQK),
                         k.reshape(bsz, seq_len, MLSTM_HEADS, DQK),
                         v.reshape(bsz, seq_len, MLSTM_HEADS, DV), ig, fg)
        hm = rmsnorm(hm, mlstm_norm_w[l].reshape(MLSTM_HEADS, 1, DV))
        hm = jnp.moveaxis(hm, 1, 2).reshape(bsz, seq_len, MLSTM_W).astype(h.dtype)
        hm = jax.nn.sigmoid(og) * hm
        hc = short_conv_group(u, gb, gc, conv_w[l])
        h = h + jnp.concatenate([hm, hc], axis=-1) @ w_out[l]
        hf = rmsnorm(h, norm_ffn_w[l])
        h = h + (jax.nn.silu(hf @ w_gate[l]) * (hf @ w_up[l])) @ w_down[l]
    out = rmsnorm(h, norm_final_w)
    return out[:, N_META:]
```

```python
from contextlib import ExitStack
import numpy as np
import concourse.bass as bass
import concourse.mybir as mybir
from concourse.bass_utils import run_bass_kernel_spmd

F32 = mybir.dt.float32
BF16 = mybir.dt.bfloat16
AF = mybir.ActivationFunctionType
ALU = mybir.AluOpType

D = 2048
KC = 16
NH = 4
DQK = 128
DV = 256
DFF = 5632
DIN = 6152
NX = 1024
NM = 16
T = NX + NM
EPS = 1e-6
GATE_CAP = 15.0
ENGS = ("sync", "scalar", "vector", "gpsimd", "tensor")

C_NMW, C_NFW, C_NFIN, C_MNW, C_CVW, C_BG, NCOLS = 0, 32, 64, 80, 96, 144, 160
FF_BLOCKS = [(0, 12), (12, 12), (24, 12), (36, 8)]


class Buf:
    __slots__ = ("name", "w", "rs", "excl")

    def __init__(self, name, excl=False):
        self.name = name
        self.w = None
        self.rs = []
        self.excl = excl


class Op:
    __slots__ = ("eng", "fn", "deps", "dma_key", "signal", "count", "group")

    def __init__(self, eng, fn, dma_key, group):
        self.eng = eng
        self.fn = fn
        self.deps = []
        self.dma_key = dma_key
        self.signal = dma_key is not None
        self.count = None
        self.group = group


class Prog:
    def __init__(self):
        self.ops = {e: [] for e in ENGS}

    def op(self, eng, fn, reads=(), writes=(), dma_key=None, group=None):
        o = Op(eng, fn, dma_key, group)
        deps = {}

        def add(p):
            if p is None:
                return
            if p.eng == "tensor" and eng == "tensor":
                return
            if group is not None and p.group == group:
                return
            deps[id(p)] = p

        for r in reads:
            add(r.w)
            if r.excl:
                for rd in r.rs:
                    if rd.eng != eng:
                        add(rd)
        for w in writes:
            add(w.w)
            for rd in w.rs:
                if rd.eng == eng and rd.dma_key is None and dma_key is None:
                    continue
                add(rd)
        for w in writes:
            w.w = o
            w.rs = []
        for r in reads:
            if dma_key is None:
                r.rs = [rd for rd in r.rs if rd.eng != eng or rd.dma_key is not None]
            r.rs.append(o)
        o.deps = list(deps.values())
        for p in o.deps:
            p.signal = True
        self.ops[eng].append(o)
        return o

    def dma_keys(self):
        ks = set()
        for e in ENGS:
            for o in self.ops[e]:
                if o.dma_key is not None:
                    ks.add(o.dma_key)
        return sorted(ks)

    def emit(self, sems):
        cnt = {}
        for e in ENGS:
            for o in self.ops[e]:
                if not o.signal:
                    continue
                key = ("dma", o.dma_key) if o.dma_key is not None else ("eng", e)
                cnt[key] = cnt.get(key, 0) + (16 if o.dma_key is not None else 1)
                o.count = (key, cnt[key])

        def run(e):
            def body(engobj):
                seen = {}
                for o in self.ops[e]:
                    need = {}
                    for p in o.deps:
                        k, v = p.count
                        if seen.get(k, 0) >= v:
                            continue
                        if need.get(k, 0) < v:
                            need[k] = v
                    for k, v in need.items():
                        engobj.wait_ge(sems[k], v)
                        seen[k] = v
                    ins = o.fn(engobj)
                    if o.signal:
                        k, v = o.count
                        ins.then_inc(sems[k], 16 if k[0] == "dma" else 1)
            return body
        return {e: run(e) for e in ENGS}


def build_nc(dbg=None, n_halves=2, n_layers=2, stop_after=None):
    dbg = dbg or []
    nc = bass.Bass("TRN2", target_bir_lowering=False)
    x_d = nc.dram_tensor("x", [2 * NX, D], F32, kind="ExternalInput").ap()
    meta_d = nc.dram_tensor("meta", [NM, D], F32, kind="ExternalInput").ap()
    w_in_d = nc.dram_tensor("w_in", [2, D, DIN], F32, kind="ExternalInput").ap()
    w_out_d = nc.dram_tensor("w_out", [2, D, D], F32, kind="ExternalInput").ap()
    w_gate_d = nc.dram_tensor("w_gate", [2, D, DFF], F32, kind="ExternalInput").ap()
    w_up_d = nc.dram_tensor("w_up", [2, D, DFF], F32, kind="ExternalInput").ap()
    w_down_d = nc.dram_tensor("w_down", [2, DFF, D], F32, kind="ExternalInput").ap()
    cols_d = nc.dram_tensor("cols", [128, NCOLS], F32, kind="ExternalInput").ap()
    cst_d = nc.dram_tensor("cst", [128, 512], F32, kind="ExternalInput").ap()
    out_d = nc.dram_tensor("out", [2 * NX, D], F32, kind="ExternalOutput").ap()
    dbg_d = {}
    for name in dbg:
        dbg_d[name] = nc.dram_tensor("dbg_" + name, [128, KC * T], F32, kind="ExternalOutput").ap()

    win_v = [w_in_d[l].rearrange("(kc p) n -> p kc n", p=128) for l in range(2)]
    wout_v = [w_out_d[l].rearrange("(kc p) n -> p kc n", p=128) for l in range(2)]
    wgate_v = [w_gate_d[l].rearrange("(kc p) n -> p kc n", p=128) for l in range(2)]
    wup_v = [w_up_d[l].rearrange("(kc p) n -> p kc n", p=128) for l in range(2)]
    wdown_v = [w_down_d[l].rearrange("(kc p) n -> p kc n", p=128) for l in range(2)]

    P = Prog()
    es = ExitStack()
    with es:
        sb = lambda name, shape, dt: es.enter_context(nc.sbuf_tensor(name, shape, dt))
        hT = sb("hT", [128, KC, T], F32)
        hb = sb("hb", [128, KC, T], BF16)
        mx = sb("mx", [128, KC, T], BF16)
        wb = [sb(f"wb{i}", [128, KC, 512], BF16) for i in range(2)]
        SW = 1044
        scr = [sb(f"scr{i}", [128, SW], F32) for i in range(3)]
        cst = sb("cst_sb", [128, 512], F32)
        cols = sb("cols_sb", [128, NCOLS], F32)
        ident_bf = sb("ident_bf", [128, 128], BF16)
        ones_bf = sb("ones_bf", [128, 128], BF16)
        eps_col = sb("eps_col", [128, 1], F32)
        one_col = sb("one_col", [128, 1], F32)
        rstd = sb("rstd", [128, T], F32)
        wg = sb("wg", [128, KC, 8], BF16)
        graw = sb("graw", [128, 9, 8], F32)
        gth = sb("gth", [128, 9, 8], F32)
        gli = sb("gli", [128, 9, 4], F32)
        glf = sb("glf", [128, 9, 4], F32)
        gcol = sb("gcol", [128, 9, 4], F32)
        bcol = sb("bcol", [128, 9, 4], F32)
        gres = sb("gres", [128, 9, 4], F32)
        glf3 = sb("glf3", [128, 3, 9, 4], BF16)
        U_bf = sb("U_bf", [128, 128], BF16)
        cstate = sb("cstate", [128, 2 * NH, DV + 1], F32)
        cbf = sb("cbf", [128, DV + 2], BF16)
        ctail = sb("ctail", [128, 2 * 8, 2], F32)
        lfrep = sb("lfrep", [128, 128], F32)
        ebc = sb("ebc", [128, 128], F32)
        tmpm = sb("tmpm", [128, 128], F32)
        atm = sb("atm", [128, 128], F32)
        swt = sb("swt", [128, 128], BF16)
        qtil = sb("qtil", [128, 128], BF16)
        ktil = sb("ktil", [128, 128], BF16)
        hmn = sb("hmn", [128, DV], BF16)
        junk = sb("junk", [128, DV], BF16)
        ep = sb("ep", [128, 8], F32)
        banks = [es.enter_context(nc.psum_tensor(f"bank{i}", [128, 512], F32)) for i in range(8)]

        ident_f = cst[:, 0:128]
        U_f = cst[:, 128:256]
        maskneg = cst[:, 256:384]
        ones_f = cst[:, 384:512]

        b_hT = [[Buf(f"hT{k}_{g}") for g in range(3)] for k in range(KC)]
        b_hb = [Buf(f"hb{k}") for k in range(KC)]
        b_mx = [Buf(f"mx{k}") for k in range(KC)]
        b_wb = [Buf("wb0"), Buf("wb1")]
        b_scr = [Buf(f"scr{i}") for i in range(3)]
        b_bank = [Buf(f"bank{i}", excl=True) for i in range(8)]
        b_cst, b_cols, b_identbf, b_onesbf, b_eps = Buf("cst"), Buf("cols"), Buf("identbf"), Buf("onesbf"), Buf("eps")
        b_rstd, b_wg, b_graw, b_gth, b_gli, b_glf, b_gcol = (Buf(n) for n in ("rstd", "wg", "graw", "gth", "gli", "glf", "gcol"))
        b_cstate = [Buf(f"cst{i}") for i in range(2 * NH)]
        b_cbf, b_ctail = Buf("cbf"), [Buf(f"ctail{i}") for i in range(16)]
        b_lfrep, b_ebc, b_tmpm, b_atm, b_swt, b_qtil, b_ktil, b_hmn, b_junk, b_ep = (
            Buf(n) for n in ("lfrep", "ebc", "tmpm", "atm", "swt", "qtil", "ktil", "hmn", "junk", "ep"))
        b_vt = Buf("vtok")
        b_bcol, b_gres, b_glf3 = Buf("bcol"), Buf("gres"), Buf("glf3")
        b_m0a, b_m0b, b_m0c, b_m0d = Buf("m0a"), Buf("m0b"), Buf("m0c"), [Buf("m0d0"), Buf("m0d1")]

        def hT_bufs(k, groups):
            return [b_hT[k][gi] for gi in range(len(groups))]

        def grp_of(t0):
            return 2 if t0 >= NX else t0 // 512

        def mm(out, lhsT, rhs, start, stop, reads, writes):
            P.op("tensor", lambda e: e.matmul(out, lhsT=lhsT, rhs=rhs, start=start, stop=stop), reads, writes)

        def tr(out, in_, ident, reads, writes):
            P.op("tensor", lambda e: e.transpose(out, in_, ident), reads, writes)

        def act(out, in_, func, reads, writes, bias=None, scale=None, accum_out=None):
            kw = {}
            if bias is not None:
                kw["bias"] = bias
            if scale is not None:
                kw["scale"] = scale
            if accum_out is not None:
                kw["accum_out"] = accum_out
            P.op("scalar", lambda e: e.activation(out, in_, func, **kw), reads, writes)

        def tt(out, in0, in1, op, reads, writes, eng="vector"):
            P.op(eng, lambda e: e.tensor_tensor(out, in0, in1, op), reads, writes)

        def stt(out, in0, scalar, in1, op0, op1, reads, writes):
            P.op("vector", lambda e: e.scalar_tensor_tensor(out, in0, scalar, in1, op0, op1), reads, writes)

        def ts(out, in0, s1, s2, op0, op1, reads, writes, eng="vector"):
            if s2 is None:
                P.op(eng, lambda e: e.tensor_scalar(out, in0, s1, None, op0), reads, writes)
            else:
                P.op(eng, lambda e: e.tensor_scalar(out, in0, s1, s2, op0, op1), reads, writes)

        def cp(out, in_, reads, writes, eng="vector"):
            if eng == "scalar":
                act(out, in_, AF.Copy, reads, writes)
            else:
                P.op(eng, lambda e: e.tensor_copy(out, in_), reads, writes)

        def memset(ap, val, writes, eng="vector"):
            P.op(eng, lambda e: e.memset(ap, val), (), writes)

        def dma(eng, out, in_, reads, writes, key, group=None):
            P.op(eng, lambda e: e.dma_start(out=out, in_=in_), reads, writes, dma_key=key, group=group)

        class _Stop(Exception):
            pass

        cur = {"hf": 0}

        def chk(stage):
            if stop_after == stage or stop_after == f"{stage}@{cur['hf']}":
                raise _Stop()

        dump_i = [0]

        def dump(name, src_ap, reads):
            if name in dbg_d:
                tcur = T if cur["hf"] == 0 else NX
                dst = dbg_d[name][:, :].rearrange("p (a b) -> p a b", a=KC)[:, :, 0:tcur]
                dump_i[0] += 1
                dma("gpsimd", dst, src_ap[:, :, 0:tcur], reads, (), f"dbg{dump_i[0]}")

        wstate = {"i": 0, "g": 0}

        def wload(pieces, kcn=KC):
            s = wstate["i"] % 2
            wstate["i"] += 1
            wstate["g"] += 1
            for (off, ncol, src) in pieces:
                dma("gpsimd", wb[s][:, 0:kcn, off:off + ncol], src, (), [b_wb[s]], f"wb{s}", group=("w", wstate["g"]))
            return s

        ring = {"i": 0}
        NRING = 5

        def next_bank():
            b = ring["i"] % NRING
            ring["i"] += 1
            return b

        def big_mm(slot, coloff, rhs, rhs_bufs, kcn, groups):
            bks = [next_bank() for _ in groups]
            for k in range(kcn):
                for gi, (t0, tn) in enumerate(groups):
                    mm(banks[bks[gi]][:, 0:tn], wb[slot][:, k, coloff:coloff + 128], rhs[:, k, t0:t0 + tn],
                       k == 0, k == kcn - 1, [b_wb[slot], rhs_bufs[k]], [b_bank[bks[gi]]])
            return bks

        ev = {"i": 0}

        def evac_engine():
            ev["i"] += 1
            return "scalar" if ev["i"] % 2 else "vector"

        dma("sync", cst[:, :], cst_d[:, :], (), [b_cst], "c_cst")
        dma("sync", cols[:, :], cols_d[:, :], (), [b_cols], "c_cols")
        cp(ident_bf[:, :], ident_f, [b_cst], [b_identbf])
        cp(ones_bf[:, :], ones_f, [b_cst], [b_onesbf])
        cp(U_bf[:, :], U_f, [b_cst], [b_identbf])
        memset(bcol[:, :, :], 0.0, [b_bcol])
        memset(eps_col[:, :], EPS, [b_eps])
        memset(one_col[:, :], 1.0, [b_eps])
        memset(graw[:, :, :], 0.0, [b_graw])

        nstate = {"pend": None}

        def norm_feed_act(k, groups, Tc):
            sq = scr[k % 2][:, 0:T // 2 + 4].bitcast(BF16)
            act(sq[:, 0:Tc], hT[:, k, 0:Tc], AF.Square, hT_bufs(k, groups), [b_scr[k % 2]])
            return sq

        def norm_feed_mm(k, sq, groups):
            for gi, (t0, tn) in enumerate(groups):
                mm(banks[5 + gi][:, 0:tn], ones_bf[:, :], sq[:, t0:t0 + tn], k == 0, k == KC - 1,
                   [b_onesbf, b_scr[k % 2]], [b_bank[5 + gi]])

        def norm_feed(k, groups, Tc, delay=False):
            sq = norm_feed_act(k, groups, Tc)
            if delay:
                flush_norm(groups)
                nstate["pend"] = (k, sq)
            else:
                norm_feed_mm(k, sq, groups)

        def flush_norm(groups):
            if nstate["pend"] is not None:
                k, sq = nstate["pend"]
                nstate["pend"] = None
                norm_feed_mm(k, sq, groups)

        def norm_finish(wcol0, groups, Tc, out_is_hb=True, Tout=None):
            flush_norm(groups)
            for gi, (t0, tn) in enumerate(groups):
                act(rstd[:, t0:t0 + tn], banks[5 + gi][:, 0:tn], AF.Ln, [b_bank[5 + gi], b_eps], [b_rstd],
                    bias=eps_col[:, 0:1], scale=1.0 / D)
            act(rstd[:, 0:Tc], rstd[:, 0:Tc], AF.Exp, [b_rstd], [b_rstd], scale=-0.5)
            Tout = Tc if Tout is None else Tout
            for k in range(KC):
                if out_is_hb:
                    stt(hb[:, k, 0:Tout], hT[:, k, 0:Tout], cols[:, wcol0 + k:wcol0 + k + 1], rstd[:, 0:Tout],
                        ALU.mult, ALU.mult, hT_bufs(k, groups) + [b_cols, b_rstd], [b_hb[k]])
                else:
                    stt(hT[:, k, 0:Tout], hT[:, k, 0:Tout], cols[:, wcol0 + k:wcol0 + k + 1], rstd[:, 0:Tout],
                        ALU.mult, ALU.mult, [b_cols, b_rstd], hT_bufs(k, groups))

        def stage_view(i):
            return mx[:, 4 * i:4 * i + 4, :].rearrange("p a b -> p (a b)").bitcast(F32)

        def stage_bufs(i):
            return [b_mx[4 * i + j] for j in range(4)]

        def load_x(hf, groups, tiles, Tc):
            si = 0
            for (t0, n) in tiles:
                st = stage_view(si % 2)
                sbuf = stage_bufs(si % 2)
                if t0 >= NX:
                    dma("sync", st[0:n, 0:D], meta_d[:, :], (), sbuf, f"st{si % 2}")
                    bk = next_bank()
                    for k in range(KC):
                        tr(banks[bk][:, n * k:n * k + n], st[0:n, 128 * k:128 * k + 128], ident_f[0:n, 0:n],
                           sbuf + [b_cst], [b_bank[bk]])
                    cp(hT[:, :, t0:t0 + n], banks[bk][:, 0:n * KC].rearrange("p (a b) -> p a b", a=KC),
                       [b_bank[bk]], [b_hT[k][2] for k in range(KC)], eng=evac_engine())
                else:
                    r0 = hf * NX + t0
                    dma("sync", st[:, 0:D], x_d[r0:r0 + 128, :], (), sbuf, f"st{si % 2}")
                    g = grp_of(t0)
                    for kq in range(4):
                        bk = next_bank()
                        for j in range(4):
                            k = 4 * kq + j
                            tr(banks[bk][:, 128 * j:128 * j + 128], st[:, 128 * k:128 * k + 128], ident_f,
                               sbuf + [b_cst], [b_bank[bk]])
                        cp(hT[:, 4 * kq:4 * kq + 4, t0:t0 + 128],
                           banks[bk][:, :].rearrange("p (a b) -> p a b", a=4),
                           [b_bank[bk]], [b_hT[4 * kq + j][g] for j in range(4)], eng=evac_engine())
                si += 1

        def store_out(hf, groups):
            si = 0
            for i in range(8):
                t0 = 128 * i
                g = grp_of(t0)
                st = stage_view(si % 2)
                sbuf = stage_bufs(si % 2)
                for kq in range(4):
                    bk = next_bank()
                    for j in range(4):
                        k = 4 * kq + j
                        tr(banks[bk][:, 128 * j:128 * j + 128], hT[:, k, t0:t0 + 128], ident_f,
                           [b_hT[k][g], b_cst], [b_bank[bk]])
                    cp(st[:, 512 * kq:512 * kq + 512], banks[bk][:, :], [b_bank[bk]], sbuf, eng=evac_engine())
                r0 = hf * NX + t0
                dma("sync", out_d[r0:r0 + 128, :], st[:, 0:D], sbuf, (), f"so{si % 2}")
                si += 1

        def conv_phase(l, hf, groups, Tc):
            u_sb, abuf, cbuf = scr[0], scr[1], scr[2]
            xoff = 2 + (NM if hf == 0 else 0)

            def aoff(t0):
                return 2 if t0 >= NX else xoff + t0

            for c in range(8):
                s = wload([(0, 128, win_v[l][:, :, 3080 + 128 * c:3080 + 128 * c + 128]),
                           (128, 128, win_v[l][:, :, 4104 + 128 * c:4104 + 128 * c + 128]),
                           (256, 128, win_v[l][:, :, 5128 + 128 * c:5128 + 128 * c + 128])])
                bk = big_mm(s, 0, hb, b_hb, KC, groups)
                for gi, (t0, tn) in enumerate(groups):
                    act(u_sb[:, t0:t0 + tn], banks[bk[gi]][:, 0:tn], AF.Copy, [b_bank[bk[gi]]], [b_scr[0]])
                if hf == 0:
                    memset(abuf[:, 0:2], 0.0, [b_scr[1]])
                else:
                    cp(abuf[:, 0:2], ctail[:, l * 8 + c, :], [b_ctail[l * 8 + c]], [b_scr[1]])
                bk = big_mm(s, 256, hb, b_hb, KC, groups)
                for gi, (t0, tn) in enumerate(groups):
                    tt(abuf[:, aoff(t0):aoff(t0) + tn], banks[bk[gi]][:, 0:tn], u_sb[:, t0:t0 + tn], ALU.mult,
                       [b_bank[bk[gi]], b_scr[0]], [b_scr[1]])
                bk = big_mm(s, 128, hb, b_hb, KC, groups)
                w0 = cols[:, C_CVW + l * 24 + 0 * 8 + c:C_CVW + l * 24 + 0 * 8 + c + 1]
                w1 = cols[:, C_CVW + l * 24 + 1 * 8 + c:C_CVW + l * 24 + 1 * 8 + c + 1]
                w2 = cols[:, C_CVW + l * 24 + 2 * 8 + c:C_CVW + l * 24 + 2 * 8 + c + 1]
                ts(cbuf[:, 0:Tc], abuf[:, 2:2 + Tc], w2, None, ALU.mult, None, [b_scr[1], b_cols], [b_scr[2]])
                stt(cbuf[:, 0:Tc], abuf[:, 1:1 + Tc], w1, cbuf[:, 0:Tc], ALU.mult, ALU.add, [b_scr[1], b_cols], [b_scr[2]])
                stt(cbuf[:, 0:Tc], abuf[:, 0:Tc], w0, cbuf[:, 0:Tc], ALU.mult, ALU.add, [b_scr[1], b_cols], [b_scr[2]])
                cp(ctail[:, l * 8 + c, :], abuf[:, Tc:Tc + 2], [b_scr[1]], [b_ctail[l * 8 + c]], eng="scalar")
                for gi, (t0, tn) in enumerate(groups):
                    co = aoff(t0) - 2
                    tt(mx[:, 8 + c, t0:t0 + tn], banks[bk[gi]][:, 0:tn], cbuf[:, co:co + tn], ALU.mult,
                       [b_bank[bk[gi]], b_scr[2]], [b_mx[8 + c]])

        def gates_phase(l, hf, tiles):
            dma("gpsimd", wg[:, :, :], win_v[l][:, :, 3072:3080], (), [b_wg], "wg")
            gbank = 5
            for (t0, n) in tiles:
                sl = t0 // 128
                for k in range(KC):
                    mm(banks[gbank][0:n, 8 * sl:8 * sl + 8], hb[:, k, t0:t0 + n], wg[:, k, :], k == 0, k == KC - 1,
                       [b_hb[k], b_wg], [b_bank[gbank]])
            for (t0, n) in tiles:
                sl = t0 // 128
                tt(graw[0:n, sl, :], banks[gbank][0:n, 8 * sl:8 * sl + 8], cols[0:n, C_BG + 8 * l:C_BG + 8 * l + 8],
                   ALU.add, [b_bank[gbank], b_cols], [b_graw])
            act(gth[:, :, :], graw[:, :, :], AF.Tanh, [b_graw], [b_gth], scale=1.0 / GATE_CAP)
            ts(gli[:, :, :], gth[:, :, 0:4], GATE_CAP, None, ALU.mult, None, [b_gth], [b_gli])
            act(glf[:, :, :], gth[:, :, 4:8], AF.Exp, [b_gth], [b_glf], scale=-GATE_CAP)
            act(glf[:, :, :], glf[:, :, :], AF.Ln, [b_glf, b_eps], [b_glf], bias=one_col[:, 0:1])
            ts(glf[:, :, :], glf[:, :, :], -1.0, None, ALU.mult, None, [b_glf], [b_glf])
            ts(glf3[:, 0, :, :], glf[:, :, :], 1.0, None, ALU.mult, None, [b_glf], [b_glf3])
            tt(gres[:, :, :], glf[:, :, :], glf3[:, 0, :, :], ALU.subtract, [b_glf, b_glf3], [b_gres])
            ts(glf3[:, 1, :, :], gres[:, :, :], 1.0, None, ALU.mult, None, [b_gres], [b_glf3])
            tt(gres[:, :, :], gres[:, :, :], glf3[:, 1, :, :], ALU.subtract, [b_glf3], [b_gres])
            ts(glf3[:, 2, :, :], gres[:, :, :], 1.0, None, ALU.mult, None, [b_gres], [b_glf3])
            bbank = 6
            for (t0, n) in tiles:
                sl = t0 // 128
                for i3 in range(3):
                    mm(banks[bbank][0:n, 4 * sl:4 * sl + 4], U_bf[0:n, 0:n], glf3[0:n, i3, sl, :], i3 == 0, i3 == 2,
                       [b_identbf, b_glf3], [b_bank[bbank]])
            for (t0, n) in tiles:
                sl = t0 // 128
                cp(bcol[0:n, sl, :], banks[bbank][0:n, 4 * sl:4 * sl + 4], [b_bank[bbank]], [b_bcol])
            for (t0, n) in tiles:
                sl = t0 // 128
                tt(gcol[0:n, sl, :], gli[0:n, sl, :], bcol[0:n, sl, :], ALU.subtract, [b_gli, b_bcol], [b_gcol])

        def mlstm_head(l, hf, h, hh, tiles, qT, kT, ogs, vtok):
            sidx = l * NH + h
            cst_h = cstate[:, sidx, :]
            nw0 = C_MNW + l * 8 + 2 * h
            MB0, MB1, MB2 = 5, 6, 7
            MBS, MBK, MBH = 0, 1, (2, 3)
            if hf > 0:
                cp(cbf[:, 0:DV + 1], cst_h, [b_cstate[sidx]], [b_cbf], eng="scalar")
            for ti, (t0, n) in enumerate(tiles):
                sl = t0 // 128
                first = (hf == 0 and ti == 0)
                ts(lfrep[0:n, :], ones_f[0:n, :], bcol[0:n, sl, h:h + 1], None, ALU.mult, None, [b_cst, b_bcol], [b_lfrep])
                chk(f"m_a{ti}")
                tr(banks[MB0][:, 0:n], lfrep[0:n, :], ident_f[0:n, 0:n], [b_lfrep, b_cst], [b_bank[MB0]])
                chk(f"m_b{ti}")
                act(ebc[:, 0:n], banks[MB0][:, 0:n], AF.Exp, [b_bank[MB0]], [b_ebc])
                chk(f"m_c{ti}")
                ts(tmpm[0:n, 0:n], banks[MB0][0:n, 0:n], gcol[0:n, sl, h:h + 1], None, ALU.add, None,
                   [b_bank[MB0], b_gcol, b_ebc], [b_tmpm])
                tt(tmpm[0:n, 0:n], tmpm[0:n, 0:n], maskneg[0:n, 0:n], ALU.add, [b_cst], [b_tmpm])
                chk(f"m_e{ti}")
                act(atm[0:n, 0:n], tmpm[0:n, 0:n], AF.Exp, [b_tmpm], [b_atm])
                chk(f"m_f{ti}")
                mm(banks[MBS][0:n, 0:n], kT[:, t0:t0 + n], qT[:, t0:t0 + n], True, True, [b_scr[0]], [b_bank[MBS]])
                tt(swt[0:n, 0:n], banks[MBS][0:n, 0:n], atm[0:n, 0:n], ALU.mult, [b_bank[MBS], b_atm], [b_swt])
                if not first:
                    tt(qtil[:, 0:n], qT[:, t0:t0 + n], ebc[:, 0:n], ALU.mult, [b_scr[0], b_ebc], [b_qtil])
                    mm(banks[MB1][0:n, 0:DV + 1], qtil[:, 0:n], cbf[:, 0:DV + 1], True, False, [b_qtil, b_cbf], [b_bank[MB1]])
                mm(banks[MB1][0:n, 0:DV + 1], swt[0:n, 0:n], vtok[0:n, sl, hh, 0:DV + 1], first, True,
                   [b_swt, b_vt], [b_bank[MB1]])
                chk(f"m_i{ti}")
                kps = banks[MBK][:, 0:64].bitcast(BF16)
                tr(kps[0:n, :], kT[:, t0:t0 + n], ident_bf[:, :], [b_scr[0], b_identbf], [b_bank[MBK]])
                ts(ktil[0:n, :], kps[0:n, :], atm[0:n, n - 1:n], None, ALU.mult, None, [b_bank[MBK], b_atm], [b_ktil])
                mm(banks[MB2][:, 0:DV + 1], ktil[0:n, :], vtok[0:n, sl, hh, 0:DV + 1], True, True, [b_ktil, b_vt], [b_bank[MB2]])
                if first:
                    cp(cst_h, banks[MB2][:, 0:DV + 1], [b_bank[MB2]], [b_cstate[sidx]])
                else:
                    stt(cst_h, cst_h, ebc[:, n - 1:n], banks[MB2][:, 0:DV + 1], ALU.mult, ALU.add,
                        [b_ebc, b_bank[MB2]], [b_cstate[sidx]])
                cp(cbf[:, 0:DV + 1], cst_h, [b_cstate[sidx]], [b_cbf], eng="scalar")
                chk(f"m_k{ti}")
                act(junk[0:n, :], banks[MB1][0:n, 0:DV], AF.Square, [b_bank[MB1]], [b_junk, b_ep], accum_out=ep[0:n, 0:1])
                cp(ep[0:n, 1:2], banks[MB1][0:n, DV:DV + 1], [b_bank[MB1]], [b_ep])
                tt(ep[0:n, 2:3], ep[0:n, 1:2], ep[0:n, 1:2], ALU.mult, [b_ep], [b_ep])
                ts(ep[0:n, 2:3], ep[0:n, 2:3], 1.0, EPS, ALU.max, ALU.mult, [b_ep], [b_ep])
                stt(ep[0:n, 3:4], ep[0:n, 0:1], 1.0 / DV, ep[0:n, 2:3], ALU.mult, ALU.add, [b_ep], [b_ep])
                act(ep[0:n, 4:5], ep[0:n, 3:4], AF.Ln, [b_ep], [b_ep])
                act(ep[0:n, 5:6], ep[0:n, 4:5], AF.Exp, [b_ep], [b_ep], scale=-0.5)
                ts(hmn[0:n, :], banks[MB1][0:n, 0:DV], ep[0:n, 5:6], None, ALU.mult, None, [b_bank[MB1], b_ep], [b_hmn])
                chk(f"m_j{ti}")
                for j in range(2):
                    pst = banks[MBH[j]][:, 0:64].bitcast(BF16)
                    tr(pst[:, 0:n], hmn[0:n, 128 * j:128 * j + 128], ident_bf[0:n, 0:n], [b_hmn, b_identbf], [b_bank[MBH[j]]])
                    stt(mx[:, 2 * h + j, t0:t0 + n], pst[:, 0:n], cols[:, nw0 + j:nw0 + j + 1], ogs[:, j, t0:t0 + n],
                        ALU.mult, ALU.mult, [b_bank[MBH[j]], b_cols, b_scr[1]], [b_mx[2 * h + j]])
                chk(f"m_t{ti}")

        def heads_phase(l, hf, groups, tiles, Tc):
            qk = scr[0][:, 0:T].bitcast(BF16)
            qT = qk[:, 0:T]
            kT = qk[:, T:2 * T]
            ogs = scr[1][:, 0:T].bitcast(BF16).rearrange("p (a b) -> p a b", a=2)
            vtok = vt[:, :, :, :]
            for pair in range(2):
                s = wload([(0, 512, win_v[l][:, :, 1024 + 512 * pair:1024 + 512 * pair + 512])])
                for (t0, n) in tiles:
                    sl = t0 // 128
                    bk = next_bank()
                    for k in range(KC):
                        mm(banks[bk][0:n, :], hb[:, k, t0:t0 + n], wb[s][:, k, :], k == 0, k == KC - 1,
                           [b_hb[k], b_wb[s]], [b_bank[bk]])
                    cp(vtok[0:n, sl, :, 0:DV], banks[bk][0:n, :].rearrange("p (a b) -> p a b", a=2),
                       [b_bank[bk]], [b_vt], eng=evac_engine())
                chk("h_v")
                memset(vtok[:, :, :, DV:DV + 1], 1.0, [b_vt])
                chk("h_vm")
                for hh in range(2):
                    h = 2 * pair + hh
                    s = wload([(0, 128, win_v[l][:, :, 128 * h:128 * h + 128]),
                               (128, 128, win_v[l][:, :, 512 + 128 * h:512 + 128 * h + 128]),
                               (256, 256, win_v[l][:, :, 2048 + 256 * h:2048 + 256 * h + 256])])
                    bk = big_mm(s, 0, hb, b_hb, KC, groups)
                    for gi, (t0, tn) in enumerate(groups):
                        act(qT[:, t0:t0 + tn], banks[bk[gi]][:, 0:tn], AF.Copy, [b_bank[bk[gi]]], [b_scr[0]], scale=DQK ** -0.5)
                    bk = big_mm(s, 128, hb, b_hb, KC, groups)
                    for gi, (t0, tn) in enumerate(groups):
                        cp(kT[:, t0:t0 + tn], banks[bk[gi]][:, 0:tn], [b_bank[bk[gi]]], [b_scr[0]])
                    for j in range(2):
                        bk = big_mm(s, 256 + 128 * j, hb, b_hb, KC, groups)
                        for gi, (t0, tn) in enumerate(groups):
                            act(ogs[:, j, t0:t0 + tn], banks[bk[gi]][:, 0:tn], AF.Sigmoid, [b_bank[bk[gi]]], [b_scr[1]])
                    chk("h_qk")
                    mlstm_head(l, hf, h, hh, tiles, qT, kT, ogs, vtok)
                    chk("h_m0")

        def outproj_phase(l, groups, Tc):
            for blk in range(4):
                s = wload([(0, 512, wout_v[l][:, :, 512 * blk:512 * blk + 512])])
                for ft in range(4):
                    f = 4 * blk + ft
                    bk = big_mm(s, 128 * ft, mx, b_mx, KC, groups)
                    for gi, (t0, tn) in enumerate(groups):
                        tt(hT[:, f, t0:t0 + tn], banks[bk[gi]][:, 0:tn], hT[:, f, t0:t0 + tn], ALU.add,
                           [b_bank[bk[gi]]], [b_hT[f][gi]])
                    norm_feed(f, groups, Tc, delay=True)

        def ffn_phase(l, groups, Tc, feed_norm):
            sil = [scr[0], scr[1]]
            for bi, (tb, nb) in enumerate(FF_BLOCKS):
                for p in range(nb // 2):
                    i0 = tb + 2 * p
                    s = wload([(0, 256, wgate_v[l][:, :, 128 * i0:128 * i0 + 256]),
                               (256, 256, wup_v[l][:, :, 128 * i0:128 * i0 + 256])])
                    for j in range(2):
                        bk = big_mm(s, 128 * j, hb, b_hb, KC, groups)
                        for gi, (t0, tn) in enumerate(groups):
                            act(sil[j][:, t0:t0 + tn], banks[bk[gi]][:, 0:tn], AF.Silu, [b_bank[bk[gi]]], [b_scr[j]])
                        bk = big_mm(s, 256 + 128 * j, hb, b_hb, KC, groups)
                        for gi, (t0, tn) in enumerate(groups):
                            tt(mx[:, 2 * p + j, t0:t0 + tn], banks[bk[gi]][:, 0:tn], sil[j][:, t0:t0 + tn], ALU.mult,
                               [b_bank[bk[gi]], b_scr[j]], [b_mx[2 * p + j]])
                last = bi == len(FF_BLOCKS) - 1
                for cb in range(4):
                    s = wload([(0, 512, wdown_v[l][:, tb:tb + nb, 512 * cb:512 * cb + 512])], kcn=nb)
                    for ft in range(4):
                        f = 4 * cb + ft
                        bk = big_mm(s, 128 * ft, mx, b_mx, nb, groups)
                        for gi, (t0, tn) in enumerate(groups):
                            tt(hT[:, f, t0:t0 + tn], banks[bk[gi]][:, 0:tn], hT[:, f, t0:t0 + tn], ALU.add,
                               [b_bank[bk[gi]]], [b_hT[f][gi]])
                        if last and feed_norm:
                            norm_feed(f, groups, Tc, delay=True)

        vt = sb("vtok", [128, 9, 2, DV + 2], BF16)
        try:
            for hf in range(n_halves):
                groups = [(0, 512), (512, 512)] + ([(NX, NM)] if hf == 0 else [])
                tiles = ([(NX, NM)] if hf == 0 else []) + [(128 * i, 128) for i in range(8)]
                Tc = T if hf == 0 else NX
                cur["hf"] = hf
                load_x(hf, groups, tiles, Tc)
                dump(f"x{hf}", hT[:, :, :], [b for k in range(KC) for b in b_hT[k]])
                chk("load")
                for k in range(KC):
                    norm_feed(k, groups, Tc)
                for l in range(n_layers):
                    norm_finish(C_NMW + l * 16, groups, Tc)
                    dump(f"hb{hf}{l}", hb[:, :, :], b_hb)
                    chk("norm")
                    conv_phase(l, hf, groups, Tc)
                    chk("conv")
                    gates_phase(l, hf, tiles)
                    chk("gates")
                    heads_phase(l, hf, groups, tiles, Tc)
                    dump(f"mx{hf}{l}", mx[:, :, :], b_mx)
                    chk("heads")
                    outproj_phase(l, groups, Tc)
                    dump(f"hm{hf}{l}", hT[:, :, :], [b for k in range(KC) for b in b_hT[k]])
                    norm_finish(C_NFW + l * 16, groups, Tc)
                    chk("outproj")
                    ffn_phase(l, groups, Tc, feed_norm=True)
                    dump(f"hf{hf}{l}", hT[:, :, :], [b for k in range(KC) for b in b_hT[k]])
                    chk("ffn")
                norm_finish(C_NFIN, groups, Tc, out_is_hb=False, Tout=NX)
                store_out(hf, groups)
        except _Stop:
            flush_norm(groups)
            if stop_after.split("@")[0] in ("conv", "gates"):
                dump(f"mx{hf}{l}", mx[:, :, :], b_mx)
        b_fin = [Buf("fin_act"), Buf("fin_dve")]
        act(ep[0:1, 6:7], one_col[0:1, 0:1], AF.Copy, [b_eps], [b_fin[0]])
        memset(ep[0:1, 7:8], 0.0, [b_fin[1]])
        every = [b for b in b_wb + b_bank + [b_wg]]
        P.op("sync", lambda e: None, b_fin + every, stage_bufs(0) + stage_bufs(1))

        sems = {}
        for e in ENGS:
            sems[("eng", e)] = es.enter_context(nc.semaphore(f"s_{e}"))
        for k in P.dma_keys():
            sems[("dma", k)] = es.enter_context(nc.semaphore(f"d_{k}"))
        bodies = P.emit(sems)
        with nc.Block() as block:
            block.sync(bodies["sync"])
            block.scalar(bodies["scalar"])
            block.vector(bodies["vector"])
            block.gpsimd(bodies["gpsimd"])
            block.tensor(bodies["tensor"])
    return nc


def host_tables(norm_mix_w, norm_ffn_w, norm_final_w, mlstm_norm_w, conv_w, b_gates):
    cols = np.zeros((128, NCOLS), np.float32)
    for l in range(2):
        cols[:, C_NMW + 16 * l:C_NMW + 16 * l + 16] = np.asarray(norm_mix_w[l], np.float32).reshape(16, 128).T
        cols[:, C_NFW + 16 * l:C_NFW + 16 * l + 16] = np.asarray(norm_ffn_w[l], np.float32).reshape(16, 128).T
        cols[:, C_MNW + 8 * l:C_MNW + 8 * l + 8] = np.asarray(mlstm_norm_w[l], np.float32).reshape(8, 128).T
        for j in range(3):
            cols[:, C_CVW + 24 * l + 8 * j:C_CVW + 24 * l + 8 * j + 8] = np.asarray(conv_w[l, j], np.float32).reshape(8, 128).T
        cols[:, C_BG + 8 * l:C_BG + 8 * l + 8] = np.broadcast_to(np.asarray(b_gates[l], np.float32)[None, :], (128, 8))
    cols[:, C_NFIN:C_NFIN + 16] = np.asarray(norm_final_w, np.float32).reshape(16, 128).T
    cst = np.zeros((128, 512), np.float32)
    s = np.arange(128)[:, None]
    t = np.arange(128)[None, :]
    cst[:, 0:128] = np.eye(128, dtype=np.float32)
    cst[:, 128:256] = (s <= t).astype(np.float32)
    cst[:, 256:384] = np.where(s <= t, 0.0, -30000.0).astype(np.float32)
    cst[:, 384:512] = 1.0
    return cols, cst


_NC_CACHE = {}


def kernel(x, meta_tokens, norm_mix_w, w_in, b_gates, conv_w, mlstm_norm_w, w_out,
           norm_ffn_w, w_gate, w_up, w_down, norm_final_w):
    x = np.asarray(x, np.float32)
    cols, cst = host_tables(np.asarray(norm_mix_w), np.asarray(norm_ffn_w), np.asarray(norm_final_w),
                            np.asarray(mlstm_norm_w), np.asarray(conv_w), np.asarray(b_gates))
    if "nc" not in _NC_CACHE:
        _NC_CACHE["nc"] = build_nc()
    nc = _NC_CACHE["nc"]
    shared = {
        "meta": np.ascontiguousarray(np.asarray(meta_tokens, np.float32)),
        "w_in": np.ascontiguousarray(np.asarray(w_in, np.float32)),
        "w_out": np.ascontiguousarray(np.asarray(w_out, np.float32)),
        "w_gate": np.ascontiguousarray(np.asarray(w_gate, np.float32)),
        "w_up": np.ascontiguousarray(np.asarray(w_up, np.float32)),
        "w_down": np.ascontiguousarray(np.asarray(w_down, np.float32)),
        "cols": cols, "cst": cst,
    }
    in_maps = [dict(shared, x=np.ascontiguousarray(x[b])) for b in range(8)]
    res = run_bass_kernel_spmd(nc, in_maps, core_ids=list(range(8)))
    return np.stack([np.asarray(r["out"], np.float32) for r in res.results], axis=0)
```

```python
from contextlib import ExitStack
import numpy as np
import concourse.bass as bass
import concourse.mybir as mybir
from concourse.bass_utils import run_bass_kernel_spmd

F32 = mybir.dt.float32
BF16 = mybir.dt.bfloat16
AF = mybir.ActivationFunctionType
ALU = mybir.AluOpType

D = 2048
KC = 16
NH = 4
DQK = 128
DV = 256
DFF = 5632
DIN = 6152
NX = 1024
NM = 16
T = NX + NM
EPS = 1e-6
GATE_CAP = 15.0
ENGS = ("sync", "scalar", "vector", "gpsimd", "tensor")

C_NMW, C_NFW, C_NFIN, C_MNW, C_CVW, C_BG, NCOLS = 0, 32, 64, 80, 96, 144, 160
FF_BLOCKS = [(0, 12), (12, 12), (24, 12), (36, 8)]


class Buf:
    __slots__ = ("name", "w", "rs", "excl")

    def __init__(self, name, excl=False):
        self.name = name
        self.w = None
        self.rs = []
        self.excl = excl


class Op:
    __slots__ = ("eng", "fn", "deps", "dma_key", "signal", "count", "group")

    def __init__(self, eng, fn, dma_key, group):
        self.eng = eng
        self.fn = fn
        self.deps = []
        self.dma_key = dma_key
        self.signal = dma_key is not None
        self.count = None
        self.group = group


class Prog:
    def __init__(self):
        self.ops = {e: [] for e in ENGS}

    def op(self, eng, fn, reads=(), writes=(), dma_key=None, group=None):
        o = Op(eng, fn, dma_key, group)
        deps = {}

        def add(p):
            if p is None:
                return
            if p.eng == "tensor" and eng == "tensor":
                return
            if group is not None and p.group == group:
                return
            deps[id(p)] = p

        for r in reads:
            add(r.w)
            if r.excl:
                for rd in r.rs:
                    if rd.eng != eng:
                        add(rd)
        for w in writes:
            add(w.w)
            for rd in w.rs:
                if rd.eng == eng and rd.dma_key is None and dma_key is None:
                    continue
                add(rd)
        for w in writes:
            w.w = o
            w.rs = []
        for r in reads:
            if dma_key is None:
                r.rs = [rd for rd in r.rs if rd.eng != eng or rd.dma_key is not None]
            r.rs.append(o)
        o.deps = list(deps.values())
        for p in o.deps:
            p.signal = True
        self.ops[eng].append(o)
        return o

    def dma_keys(self):
        ks = set()
        for e in ENGS:
            for o in self.ops[e]:
                if o.dma_key is not None:
                    ks.add(o.dma_key)
        return sorted(ks)

    def emit(self, sems):
        cnt = {}
        for e in ENGS:
            for o in self.ops[e]:
                if not o.signal:
                    continue
                key = ("dma", o.dma_key) if o.dma_key is not None else ("eng", e)
                cnt[key] = cnt.get(key, 0) + (16 if o.dma_key is not None else 1)
                o.count = (key, cnt[key])

        def run(e):
            def body(engobj):
                seen = {}
                for o in self.ops[e]:
                    need = {}
                    for p in o.deps:
                        k, v = p.count
                        if seen.get(k, 0) >= v:
                            continue
                        if need.get(k, 0) < v:
                            need[k] = v
                    for k, v in need.items():
                        engobj.wait_ge(sems[k], v)
                        seen[k] = v
                    ins = o.fn(engobj)
                    if o.signal:
                        k, v = o.count
                        ins.then_inc(sems[k], 16 if k[0] == "dma" else 1)
            return body
        return {e: run(e) for e in ENGS}


def build_nc(dbg=None, n_halves=2, n_layers=2, stop_after=None):
    dbg = dbg or []
    nc = bass.Bass("TRN2", target_bir_lowering=False)
    x_d = nc.dram_tensor("x", [2 * NX, D], F32, kind="ExternalInput").ap()
    meta_d = nc.dram_tensor("meta", [NM, D], F32, kind="ExternalInput").ap()
    w_in_d = nc.dram_tensor("w_in", [2, D, DIN], F32, kind="ExternalInput").ap()
    w_out_d = nc.dram_tensor("w_out", [2, D, D], F32, kind="ExternalInput").ap()
    w_gate_d = nc.dram_tensor("w_gate", [2, D, DFF], F32, kind="ExternalInput").ap()
    w_up_d = nc.dram_tensor("w_up", [2, D, DFF], F32, kind="ExternalInput").ap()
    w_down_d = nc.dram_tensor("w_down", [2, DFF, D], F32, kind="ExternalInput").ap()
    cols_d = nc.dram_tensor("cols", [128, NCOLS], F32, kind="ExternalInput").ap()
    cst_d = nc.dram_tensor("cst", [128, 512], F32, kind="ExternalInput").ap()
    out_d = nc.dram_tensor("out", [2 * NX, D], F32, kind="ExternalOutput").ap()
    dbg_d = {}
    for name in dbg:
        dbg_d[name] = nc.dram_tensor("dbg_" + name, [128, KC * T], F32, kind="ExternalOutput").ap()

    win_v = [w_in_d[l].rearrange("(kc p) n -> p kc n", p=128) for l in range(2)]
    wout_v = [w_out_d[l].rearrange("(kc p) n -> p kc n", p=128) for l in range(2)]
    wgate_v = [w_gate_d[l].rearrange("(kc p) n -> p kc n", p=128) for l in range(2)]
    wup_v = [w_up_d[l].rearrange("(kc p) n -> p kc n", p=128) for l in range(2)]
    wdown_v = [w_down_d[l].rearrange("(kc p) n -> p kc n", p=128) for l in range(2)]

    P = Prog()
    es = ExitStack()
    with es:
        sb = lambda name, shape, dt: es.enter_context(nc.sbuf_tensor(name, shape, dt))
        hT = sb("hT", [128, KC, T], F32)
        hb = sb("hb", [128, KC, T], BF16)
        mx = sb("mx", [128, KC, T], BF16)
        wb = [sb(f"wb{i}", [128, KC, 512], BF16) for i in range(2)]
        SW = 1044
        scr = [sb(f"scr{i}", [128, SW], F32) for i in range(4)]
        cst = sb("cst_sb", [128, 512], F32)
        cols = sb("cols_sb", [128, NCOLS], F32)
        ident_bf = sb("ident_bf", [128, 128], BF16)
        ones_bf = sb("ones_bf", [128, 128], BF16)
        eps_col = sb("eps_col", [128, 1], F32)
        one_col = sb("one_col", [128, 1], F32)
        wg = sb("wg", [128, KC, 8], BF16)
        graw = sb("graw", [128, 9, 8], F32)
        gth = sb("gth", [128, 9, 8], F32)
        gli = sb("gli", [128, 9, 4], F32)
        glf = sb("glf", [128, 9, 4], F32)
        gcol = sb("gcol", [128, 9, 4], F32)
        bcol = sb("bcol", [128, 9, 4], F32)
        gres = sb("gres", [128, 9, 4], F32)
        glf3 = sb("glf3", [128, 3, 9, 4], BF16)
        U_bf = sb("U_bf", [128, 128], BF16)
        cstate = sb("cstate", [128, 2 * NH, DV + 1], F32)
        cbf = sb("cbf", [128, DV + 2], BF16)
        ctail = sb("ctail", [128, 2 * 8, 2], F32)
        lfrep = sb("lfrep", [128, 128], F32)
        ebc = sb("ebc", [128, 128], F32)
        tmpm = sb("tmpm", [128, 128], F32)
        atm = sb("atm", [128, 128], F32)
        swt = sb("swt", [128, 128], BF16)
        qtil = sb("qtil", [128, 128], BF16)
        ktil = sb("ktil", [128, 128], BF16)
        hmn = sb("hmn", [128, DV], BF16)
        junk = sb("junk", [128, DV], BF16)
        ep = sb("ep", [128, 8], F32)
        banks = [es.enter_context(nc.psum_tensor(f"bank{i}", [128, 512], F32)) for i in range(8)]

        rstd = scr[3]
        ident_f = cst[:, 0:128]
        U_f = cst[:, 128:256]
        maskneg = cst[:, 256:384]
        ones_f = cst[:, 384:512]

        b_hT = [[Buf(f"hT{k}_{g}") for g in range(3)] for k in range(KC)]
        b_hb = [Buf(f"hb{k}") for k in range(KC)]
        b_mx = [Buf(f"mx{k}") for k in range(KC)]
        b_wb = [Buf("wb0"), Buf("wb1")]
        b_scr = [Buf(f"scr{i}") for i in range(4)]
        b_bank = [Buf(f"bank{i}", excl=True) for i in range(8)]
        b_cst, b_cols, b_identbf, b_onesbf, b_eps = Buf("cst"), Buf("cols"), Buf("identbf"), Buf("onesbf"), Buf("eps")
        b_wg, b_graw, b_gth, b_gli, b_glf, b_gcol = (Buf(n) for n in ("wg", "graw", "gth", "gli", "glf", "gcol"))
        b_rstd = b_scr[3]
        b_cstate = [Buf(f"cst{i}") for i in range(2 * NH)]
        b_cbf, b_ctail = Buf("cbf"), [Buf(f"ctail{i}") for i in range(16)]
        b_lfrep, b_ebc, b_tmpm, b_atm, b_swt, b_qtil, b_ktil, b_hmn, b_junk, b_ep = (
            Buf(n) for n in ("lfrep", "ebc", "tmpm", "atm", "swt", "qtil", "ktil", "hmn", "junk", "ep"))
        b_vt = Buf("vtok")
        b_bcol, b_gres, b_glf3 = Buf("bcol"), Buf("gres"), Buf("glf3")
        b_m0a, b_m0b, b_m0c, b_m0d = Buf("m0a"), Buf("m0b"), Buf("m0c"), [Buf("m0d0"), Buf("m0d1")]

        def hT_bufs(k, groups):
            return [b_hT[k][gi] for gi in range(len(groups))]

        def grp_of(t0):
            return 2 if t0 >= NX else t0 // 512

        def mm(out, lhsT, rhs, start, stop, reads, writes):
            P.op("tensor", lambda e: e.matmul(out, lhsT=lhsT, rhs=rhs, start=start, stop=stop), reads, writes)

        def tr(out, in_, ident, reads, writes):
            P.op("tensor", lambda e: e.transpose(out, in_, ident), reads, writes)

        def act(out, in_, func, reads, writes, bias=None, scale=None, accum_out=None):
            kw = {}
            if bias is not None:
                kw["bias"] = bias
            if scale is not None:
                kw["scale"] = scale
            if accum_out is not None:
                kw["accum_out"] = accum_out
            P.op("scalar", lambda e: e.activation(out, in_, func, **kw), reads, writes)

        def tt(out, in0, in1, op, reads, writes, eng="vector"):
            P.op(eng, lambda e: e.tensor_tensor(out, in0, in1, op), reads, writes)

        def stt(out, in0, scalar, in1, op0, op1, reads, writes):
            P.op("vector", lambda e: e.scalar_tensor_tensor(out, in0, scalar, in1, op0, op1), reads, writes)

        def ts(out, in0, s1, s2, op0, op1, reads, writes, eng="vector"):
            if s2 is None:
                P.op(eng, lambda e: e.tensor_scalar(out, in0, s1, None, op0), reads, writes)
            else:
                P.op(eng, lambda e: e.tensor_scalar(out, in0, s1, s2, op0, op1), reads, writes)

        def cp(out, in_, reads, writes, eng="vector"):
            if eng == "scalar":
                act(out, in_, AF.Copy, reads, writes)
            else:
                P.op(eng, lambda e: e.tensor_copy(out, in_), reads, writes)

        def memset(ap, val, writes, eng="vector"):
            P.op(eng, lambda e: e.memset(ap, val), (), writes)

        def dma(eng, out, in_, reads, writes, key, group=None):
            P.op(eng, lambda e: e.dma_start(out=out, in_=in_), reads, writes, dma_key=key, group=group)

        class _Stop(Exception):
            pass

        cur = {"hf": 0}

        def chk(stage):
            if stop_after == stage or stop_after == f"{stage}@{cur['hf']}":
                raise _Stop()

        dump_i = [0]

        def dump(name, src_ap, reads):
            if name in dbg_d:
                tcur = T if cur["hf"] == 0 else NX
                dst = dbg_d[name][:, :].rearrange("p (a b) -> p a b", a=KC)[:, :, 0:tcur]
                dump_i[0] += 1
                dma("gpsimd", dst, src_ap[:, :, 0:tcur], reads, (), f"dbg{dump_i[0]}")

        wstate = {"i": 0, "g": 0}

        def wload(pieces, kcn=KC):
            s = wstate["i"] % 2
            wstate["i"] += 1
            wstate["g"] += 1
            for (off, ncol, src) in pieces:
                dma("gpsimd", wb[s][:, 0:kcn, off:off + ncol], src, (), [b_wb[s]], f"wb{s}", group=("w", wstate["g"]))
            return s

        ring = {"i": 0}
        NRING = 5

        def next_bank():
            b = ring["i"] % NRING
            ring["i"] += 1
            return b

        def big_mm(slot, coloff, rhs, rhs_bufs, kcn, groups):
            bks = [next_bank() for _ in groups]
            for k in range(kcn):
                for gi, (t0, tn) in enumerate(groups):
                    mm(banks[bks[gi]][:, 0:tn], wb[slot][:, k, coloff:coloff + 128], rhs[:, k, t0:t0 + tn],
                       k == 0, k == kcn - 1, [b_wb[slot], rhs_bufs[k]], [b_bank[bks[gi]]])
            return bks

        ev = {"i": 0}

        def evac_engine():
            ev["i"] += 1
            return "scalar" if ev["i"] % 2 else "vector"

        dma("sync", cst[:, :], cst_d[:, :], (), [b_cst], "c_cst")
        dma("sync", cols[:, :], cols_d[:, :], (), [b_cols], "c_cols")
        cp(ident_bf[:, :], ident_f, [b_cst], [b_identbf])
        cp(ones_bf[:, :], ones_f, [b_cst], [b_onesbf])
        cp(U_bf[:, :], U_f, [b_cst], [b_identbf])
        memset(bcol[:, :, :], 0.0, [b_bcol])
        memset(eps_col[:, :], EPS, [b_eps])
        memset(one_col[:, :], 1.0, [b_eps])
        memset(graw[:, :, :], 0.0, [b_graw])

        nstate = {"pend": None}

        def norm_feed_act(k, groups, Tc):
            sq = scr[k % 2][:, 0:T // 2 + 4].bitcast(BF16)
            act(sq[:, 0:Tc], hT[:, k, 0:Tc], AF.Square, hT_bufs(k, groups), [b_scr[k % 2]])
            return sq

        def norm_feed_mm(k, sq, groups):
            for gi, (t0, tn) in enumerate(groups):
                mm(banks[5 + gi][:, 0:tn], ones_bf[:, :], sq[:, t0:t0 + tn], k == 0, k == KC - 1,
                   [b_onesbf, b_scr[k % 2]], [b_bank[5 + gi]])

        def norm_feed(k, groups, Tc, delay=False):
            sq = norm_feed_act(k, groups, Tc)
            if delay:
                flush_norm(groups)
                nstate["pend"] = (k, sq)
            else:
                norm_feed_mm(k, sq, groups)

        def flush_norm(groups):
            if nstate["pend"] is not None:
                k, sq = nstate["pend"]
                nstate["pend"] = None
                norm_feed_mm(k, sq, groups)

        def norm_finish(wcol0, groups, Tc, out_is_hb=True, Tout=None):
            flush_norm(groups)
            for gi, (t0, tn) in enumerate(groups):
                act(rstd[:, t0:t0 + tn], banks[5 + gi][:, 0:tn], AF.Ln, [b_bank[5 + gi], b_eps], [b_rstd],
                    bias=eps_col[:, 0:1], scale=1.0 / D)
            act(rstd[:, 0:Tc], rstd[:, 0:Tc], AF.Exp, [b_rstd], [b_rstd], scale=-0.5)
            Tout = Tc if Tout is None else Tout
            for k in range(KC):
                if out_is_hb:
                    stt(hb[:, k, 0:Tout], hT[:, k, 0:Tout], cols[:, wcol0 + k:wcol0 + k + 1], rstd[:, 0:Tout],
                        ALU.mult, ALU.mult, hT_bufs(k, groups) + [b_cols, b_rstd], [b_hb[k]])
                else:
                    stt(hT[:, k, 0:Tout], hT[:, k, 0:Tout], cols[:, wcol0 + k:wcol0 + k + 1], rstd[:, 0:Tout],
                        ALU.mult, ALU.mult, [b_cols, b_rstd], hT_bufs(k, groups))

        def stage_view(i):
            return mx[:, 4 * i:4 * i + 4, :].rearrange("p a b -> p (a b)").bitcast(F32)

        def stage_bufs(i):
            return [b_mx[4 * i + j] for j in range(4)]

        def load_x(hf, groups, tiles, Tc):
            si = 0
            for (t0, n) in tiles:
                st = stage_view(si % 2)
                sbuf = stage_bufs(si % 2)
                if t0 >= NX:
                    dma("sync", st[0:n, 0:D], meta_d[:, :], (), sbuf, f"st{si % 2}")
                    bk = next_bank()
                    for k in range(KC):
                        tr(banks[bk][:, n * k:n * k + n], st[0:n, 128 * k:128 * k + 128], ident_f[0:n, 0:n],
                           sbuf + [b_cst], [b_bank[bk]])
                    cp(hT[:, :, t0:t0 + n], banks[bk][:, 0:n * KC].rearrange("p (a b) -> p a b", a=KC),
                       [b_bank[bk]], [b_hT[k][2] for k in range(KC)], eng=evac_engine())
                else:
                    r0 = hf * NX + t0
                    dma("sync", st[:, 0:D], x_d[r0:r0 + 128, :], (), sbuf, f"st{si % 2}")
                    g = grp_of(t0)
                    for kq in range(4):
                        bk = next_bank()
                        for j in range(4):
                            k = 4 * kq + j
                            tr(banks[bk][:, 128 * j:128 * j + 128], st[:, 128 * k:128 * k + 128], ident_f,
                               sbuf + [b_cst], [b_bank[bk]])
                        cp(hT[:, 4 * kq:4 * kq + 4, t0:t0 + 128],
                           banks[bk][:, :].rearrange("p (a b) -> p a b", a=4),
                           [b_bank[bk]], [b_hT[4 * kq + j][g] for j in range(4)], eng=evac_engine())
                si += 1

        def store_out(hf, groups):
            si = 0
            for i in range(8):
                t0 = 128 * i
                g = grp_of(t0)
                st = stage_view(si % 2)
                sbuf = stage_bufs(si % 2)
                for kq in range(4):
                    bk = next_bank()
                    for j in range(4):
                        k = 4 * kq + j
                        tr(banks[bk][:, 128 * j:128 * j + 128], hT[:, k, t0:t0 + 128], ident_f,
                           [b_hT[k][g], b_cst], [b_bank[bk]])
                    cp(st[:, 512 * kq:512 * kq + 512], banks[bk][:, :], [b_bank[bk]], sbuf, eng=evac_engine())
                r0 = hf * NX + t0
                dma("sync", out_d[r0:r0 + 128, :], st[:, 0:D], sbuf, (), f"so{si % 2}")
                si += 1

        def conv_gen(l, hf, groups, Tc):
            abuf, cbuf = scr[2], scr[3]
            b_ab, b_cb = b_scr[2], b_scr[3]
            xoff = 2 + (NM if hf == 0 else 0)

            def aoff(t0):
                return 2 if t0 >= NX else xoff + t0

            for c in range(8):
                s = wload([(0, 128, win_v[l][:, :, 3080 + 128 * c:3080 + 128 * c + 128]),
                           (128, 128, win_v[l][:, :, 4104 + 128 * c:4104 + 128 * c + 128]),
                           (256, 128, win_v[l][:, :, 5128 + 128 * c:5128 + 128 * c + 128])])
                bk = big_mm(s, 0, hb, b_hb, KC, groups)
                for gi, (t0, tn) in enumerate(groups):
                    act(abuf[:, aoff(t0):aoff(t0) + tn], banks[bk[gi]][:, 0:tn], AF.Copy, [b_bank[bk[gi]]], [b_ab])
                if hf == 0:
                    memset(abuf[:, 0:2], 0.0, [b_ab])
                else:
                    cp(abuf[:, 0:2], ctail[:, l * 8 + c, :], [b_ctail[l * 8 + c]], [b_ab])
                yield 1
                bk = big_mm(s, 256, hb, b_hb, KC, groups)
                for gi, (t0, tn) in enumerate(groups):
                    tt(abuf[:, aoff(t0):aoff(t0) + tn], banks[bk[gi]][:, 0:tn], abuf[:, aoff(t0):aoff(t0) + tn], ALU.mult,
                       [b_bank[bk[gi]]], [b_ab])
                yield 1
                bk = big_mm(s, 128, hb, b_hb, KC, groups)
                w0 = cols[:, C_CVW + l * 24 + 0 * 8 + c:C_CVW + l * 24 + 0 * 8 + c + 1]
                w1 = cols[:, C_CVW + l * 24 + 1 * 8 + c:C_CVW + l * 24 + 1 * 8 + c + 1]
                w2 = cols[:, C_CVW + l * 24 + 2 * 8 + c:C_CVW + l * 24 + 2 * 8 + c + 1]
                ts(cbuf[:, 0:Tc], abuf[:, 2:2 + Tc], w2, None, ALU.mult, None, [b_ab, b_cols], [b_cb])
                stt(cbuf[:, 0:Tc], abuf[:, 1:1 + Tc], w1, cbuf[:, 0:Tc], ALU.mult, ALU.add, [b_ab, b_cols], [b_cb])
                stt(cbuf[:, 0:Tc], abuf[:, 0:Tc], w0, cbuf[:, 0:Tc], ALU.mult, ALU.add, [b_ab, b_cols], [b_cb])
                cp(ctail[:, l * 8 + c, :], abuf[:, Tc:Tc + 2], [b_ab], [b_ctail[l * 8 + c]], eng="scalar")
                for gi, (t0, tn) in enumerate(groups):
                    co = aoff(t0) - 2
                    tt(mx[:, 8 + c, t0:t0 + tn], banks[bk[gi]][:, 0:tn], cbuf[:, co:co + tn], ALU.mult,
                       [b_bank[bk[gi]], b_cb], [b_mx[8 + c]])
                yield 1

        def gates_phase(l, hf, tiles):
            dma("gpsimd", wg[:, :, :], win_v[l][:, :, 3072:3080], (), [b_wg], "wg")
            gbank = 5
            for (t0, n) in tiles:
                sl = t0 // 128
                for k in range(KC):
                    mm(banks[gbank][0:n, 8 * sl:8 * sl + 8], hb[:, k, t0:t0 + n], wg[:, k, :], k == 0, k == KC - 1,
                       [b_hb[k], b_wg], [b_bank[gbank]])
            for (t0, n) in tiles:
                sl = t0 // 128
                tt(graw[0:n, sl, :], banks[gbank][0:n, 8 * sl:8 * sl + 8], cols[0:n, C_BG + 8 * l:C_BG + 8 * l + 8],
                   ALU.add, [b_bank[gbank], b_cols], [b_graw])
            act(gth[:, :, :], graw[:, :, :], AF.Tanh, [b_graw], [b_gth], scale=1.0 / GATE_CAP)
            ts(gli[:, :, :], gth[:, :, 0:4], GATE_CAP, None, ALU.mult, None, [b_gth], [b_gli])
            act(glf[:, :, :], gth[:, :, 4:8], AF.Exp, [b_gth], [b_glf], scale=-GATE_CAP)
            act(glf[:, :, :], glf[:, :, :], AF.Ln, [b_glf, b_eps], [b_glf], bias=one_col[:, 0:1])
            ts(glf[:, :, :], glf[:, :, :], -1.0, None, ALU.mult, None, [b_glf], [b_glf])
            ts(glf3[:, 0, :, :], glf[:, :, :], 1.0, None, ALU.mult, None, [b_glf], [b_glf3])
            tt(gres[:, :, :], glf[:, :, :], glf3[:, 0, :, :], ALU.subtract, [b_glf, b_glf3], [b_gres])
            ts(glf3[:, 1, :, :], gres[:, :, :], 1.0, None, ALU.mult, None, [b_gres], [b_glf3])
            tt(gres[:, :, :], gres[:, :, :], glf3[:, 1, :, :], ALU.subtract, [b_glf3], [b_gres])
            ts(glf3[:, 2, :, :], gres[:, :, :], 1.0, None, ALU.mult, None, [b_gres], [b_glf3])
            bbank = 6
            for (t0, n) in tiles:
                sl = t0 // 128
                for i3 in range(3):
                    mm(banks[bbank][0:n, 4 * sl:4 * sl + 4], U_bf[0:n, 0:n], glf3[0:n, i3, sl, :], i3 == 0, i3 == 2,
                       [b_identbf, b_glf3], [b_bank[bbank]])
            for (t0, n) in tiles:
                sl = t0 // 128
                cp(bcol[0:n, sl, :], banks[bbank][0:n, 4 * sl:4 * sl + 4], [b_bank[bbank]], [b_bcol])
            for (t0, n) in tiles:
                sl = t0 // 128
                tt(gcol[0:n, sl, :], gli[0:n, sl, :], bcol[0:n, sl, :], ALU.subtract, [b_gli, b_bcol], [b_gcol])

        def mlstm_slots(l, hf, h, hh, tiles, qT, kT, ogs, vtok):
            sidx = l * NH + h
            cst_h = cstate[:, sidx, :]
            nw0 = C_MNW + l * 8 + 2 * h
            MBA, MBN, MBC = 5, 6, 7
            bA, bN, bC = b_bank[MBA], b_bank[MBN], b_bank[MBC]
            Bbc_r = banks[MBA][:, 0:128]
            kps_r = banks[MBA][:, 128:192].bitcast(BF16)
            ST_r = banks[MBA][:, 256:384]
            hmT_r = [banks[MBA][:, 384 + 64 * j:448 + 64 * j].bitcast(BF16) for j in range(2)]
            num_r = banks[MBN]
            dC_r = banks[MBC]
            nt = len(tiles)
            if hf > 0:
                cp(cbf[:, 0:DV + 1], cst_h, [b_cstate[sidx]], [b_cbf], eng="scalar")

            def lfrep_op(c):
                t0, n = tiles[c]
                ts(lfrep[0:n, :], ones_f[0:n, :], bcol[0:n, t0 // 128, h:h + 1], None, ALU.mult, None, [b_cst, b_bcol], [b_lfrep])

            lfrep_op(0)
            for s in range(nt + 2):
                cC, cB, cA = s - 2, s - 1, s
                hasC = 0 <= cC < nt
                hasB = 0 <= cB < nt
                hasA = cA < nt
                if hasC:
                    t0, n = tiles[cC]
                    for j in range(2):
                        tr(hmT_r[j][:, 0:n], hmn[0:n, 128 * j:128 * j + 128], ident_bf[0:n, 0:n], [b_hmn, b_identbf], [bA])
                if hasB:
                    t0, n = tiles[cB]
                    sl = t0 // 128
                    firstB = (hf == 0 and cB == 0)
                    if not firstB:
                        mm(num_r[0:n, 0:DV + 1], qtil[:, 0:n], cbf[:, 0:DV + 1], True, False, [b_qtil, b_cbf], [bN])
                    mm(num_r[0:n, 0:DV + 1], swt[0:n, 0:n], vtok[0:n, sl, hh, 0:DV + 1], firstB, True, [b_swt, b_vt], [bN])
                    mm(dC_r[:, 0:DV + 1], ktil[0:n, :], vtok[0:n, sl, hh, 0:DV + 1], True, True, [b_ktil, b_vt], [bC])
                if hasA:
                    t0, n = tiles[cA]
                    tr(Bbc_r[:, 0:n], lfrep[0:n, :], ident_f[0:n, 0:n], [b_lfrep, b_cst], [bA])
                    mm(ST_r[0:n, 0:n], kT[:, t0:t0 + n], qT[:, t0:t0 + n], True, True, [b_scr[0]], [bA])
                    tr(kps_r[0:n, :], kT[:, t0:t0 + n], ident_bf[:, :], [b_scr[0], b_identbf], [bA])
                if hasB:
                    t0, n = tiles[cB]
                    if firstB:
                        cp(cst_h, dC_r[:, 0:DV + 1], [bC], [b_cstate[sidx]])
                    else:
                        stt(cst_h, cst_h, ebc[:, n - 1:n], dC_r[:, 0:DV + 1], ALU.mult, ALU.add, [b_ebc, bC], [b_cstate[sidx]])
                    cp(cbf[:, 0:DV + 1], cst_h, [b_cstate[sidx]], [b_cbf], eng="scalar")
                if hasA:
                    t0, n = tiles[cA]
                    sl = t0 // 128
                    firstA = (hf == 0 and cA == 0)
                    act(ebc[:, 0:n], Bbc_r[:, 0:n], AF.Exp, [bA], [b_ebc])
                    ts(tmpm[0:n, 0:n], Bbc_r[0:n, 0:n], gcol[0:n, sl, h:h + 1], None, ALU.add, None, [bA, b_gcol], [b_tmpm])
                    tt(tmpm[0:n, 0:n], tmpm[0:n, 0:n], maskneg[0:n, 0:n], ALU.add, [b_cst], [b_tmpm])
                if hasC:
                    t0, n = tiles[cC]
                    for j in range(2):
                        stt(mx[:, 2 * h + j, t0:t0 + n], hmT_r[j][:, 0:n], cols[:, nw0 + j:nw0 + j + 1], ogs[:, j, t0:t0 + n],
                            ALU.mult, ALU.mult, [bA, b_cols, b_scr[1]], [b_mx[2 * h + j]])
                if hasA:
                    t0, n = tiles[cA]
                    act(atm[0:n, 0:n], tmpm[0:n, 0:n], AF.Exp, [b_tmpm], [b_atm])
                    tt(swt[0:n, 0:n], ST_r[0:n, 0:n], atm[0:n, 0:n], ALU.mult, [bA, b_atm], [b_swt])
                    ts(ktil[0:n, :], kps_r[0:n, :], atm[0:n, n - 1:n], None, ALU.mult, None, [bA, b_atm], [b_ktil])
                    if not firstA:
                        tt(qtil[:, 0:n], qT[:, t0:t0 + n], ebc[:, 0:n], ALU.mult, [b_scr[0], b_ebc], [b_qtil])
                if hasB:
                    t0, n = tiles[cB]
                    act(junk[0:n, :], num_r[0:n, 0:DV], AF.Square, [bN], [b_junk, b_ep], accum_out=ep[0:n, 0:1])
                    cp(ep[0:n, 1:2], num_r[0:n, DV:DV + 1], [bN], [b_ep])
                    tt(ep[0:n, 2:3], ep[0:n, 1:2], ep[0:n, 1:2], ALU.mult, [b_ep], [b_ep])
                    ts(ep[0:n, 2:3], ep[0:n, 2:3], 1.0, EPS, ALU.max, ALU.mult, [b_ep], [b_ep])
                    stt(ep[0:n, 3:4], ep[0:n, 0:1], 1.0 / DV, ep[0:n, 2:3], ALU.mult, ALU.add, [b_ep], [b_ep])
                    act(ep[0:n, 4:5], ep[0:n, 3:4], AF.Ln, [b_ep], [b_ep])
                    act(ep[0:n, 5:6], ep[0:n, 4:5], AF.Exp, [b_ep], [b_ep], scale=-0.5)
                    ts(hmn[0:n, :], num_r[0:n, 0:DV], ep[0:n, 5:6], None, ALU.mult, None, [bN, b_ep], [b_hmn])
                if cA + 1 < nt:
                    lfrep_op(cA + 1)
                yield

        def heads_phase(l, hf, groups, tiles, Tc):
            conv = conv_gen(l, hf, groups, Tc)
            cstate_ = {"mid": 0}
            qk = scr[0][:, 0:T].bitcast(BF16)
            qT = qk[:, 0:T]
            kT = qk[:, T:2 * T]
            ogs = scr[1][:, 0:T].bitcast(BF16).rearrange("p (a b) -> p a b", a=2)
            vtok = vt[:, :, :, :]
            for pair in range(2):
                s = wload([(0, 512, win_v[l][:, :, 1024 + 512 * pair:1024 + 512 * pair + 512])])
                for (t0, n) in tiles:
                    sl = t0 // 128
                    bk = next_bank()
                    for k in range(KC):
                        mm(banks[bk][0:n, :], hb[:, k, t0:t0 + n], wb[s][:, k, :], k == 0, k == KC - 1,
                           [b_hb[k], b_wb[s]], [b_bank[bk]])
                    cp(vtok[0:n, sl, :, 0:DV], banks[bk][0:n, :].rearrange("p (a b) -> p a b", a=2),
                       [b_bank[bk]], [b_vt], eng=evac_engine())
                chk("h_v")
                memset(vtok[:, :, :, DV:DV + 1], 1.0, [b_vt])
                chk("h_vm")
                for hh in range(2):
                    h = 2 * pair + hh
                    s = wload([(0, 128, win_v[l][:, :, 128 * h:128 * h + 128]),
                               (128, 128, win_v[l][:, :, 512 + 128 * h:512 + 128 * h + 128]),
                               (256, 256, win_v[l][:, :, 2048 + 256 * h:2048 + 256 * h + 256])])
                    bk = big_mm(s, 0, hb, b_hb, KC, groups)
                    for gi, (t0, tn) in enumerate(groups):
                        act(qT[:, t0:t0 + tn], banks[bk[gi]][:, 0:tn], AF.Copy, [b_bank[bk[gi]]], [b_scr[0]], scale=DQK ** -0.5)
                    bk = big_mm(s, 128, hb, b_hb, KC, groups)
                    for gi, (t0, tn) in enumerate(groups):
                        cp(kT[:, t0:t0 + tn], banks[bk[gi]][:, 0:tn], [b_bank[bk[gi]]], [b_scr[0]])
                    for j in range(2):
                        bk = big_mm(s, 256 + 128 * j, hb, b_hb, KC, groups)
                        for gi, (t0, tn) in enumerate(groups):
                            act(ogs[:, j, t0:t0 + tn], banks[bk[gi]][:, 0:tn], AF.Sigmoid, [b_bank[bk[gi]]], [b_scr[1]])
                    nslots = len(tiles) + 2
                    for si, _ in enumerate(mlstm_slots(l, hf, h, hh, tiles, qT, kT, ogs, vtok)):
                        if cstate_["mid"] > 0 or nslots - si >= 3:
                            if next(conv, None) is not None:
                                cstate_["mid"] = (cstate_["mid"] + 1) % 3

            for _ in conv:
                pass

        def outproj_phase(l, groups, Tc):
            for blk in range(4):
                s = wload([(0, 512, wout_v[l][:, :, 512 * blk:512 * blk + 512])])
                for ft in range(4):
                    f = 4 * blk + ft
                    bk = big_mm(s, 128 * ft, mx, b_mx, KC, groups)
                    for gi, (t0, tn) in enumerate(groups):
                        tt(hT[:, f, t0:t0 + tn], banks[bk[gi]][:, 0:tn], hT[:, f, t0:t0 + tn], ALU.add,
                           [b_bank[bk[gi]]], [b_hT[f][gi]])
                    norm_feed(f, groups, Tc, delay=True)

        def ffn_phase(l, groups, Tc, feed_norm):
            sil = [scr[0], scr[1]]
            for bi, (tb, nb) in enumerate(FF_BLOCKS):
                for p in range(nb // 2):
                    i0 = tb + 2 * p
                    s = wload([(0, 256, wgate_v[l][:, :, 128 * i0:128 * i0 + 256]),
                               (256, 256, wup_v[l][:, :, 128 * i0:128 * i0 + 256])])
                    for j in range(2):
                        bk = big_mm(s, 128 * j, hb, b_hb, KC, groups)
                        for gi, (t0, tn) in enumerate(groups):
                            act(sil[j][:, t0:t0 + tn], banks[bk[gi]][:, 0:tn], AF.Silu, [b_bank[bk[gi]]], [b_scr[j]])
                        bk = big_mm(s, 256 + 128 * j, hb, b_hb, KC, groups)
                        for gi, (t0, tn) in enumerate(groups):
                            tt(mx[:, 2 * p + j, t0:t0 + tn], banks[bk[gi]][:, 0:tn], sil[j][:, t0:t0 + tn], ALU.mult,
                               [b_bank[bk[gi]], b_scr[j]], [b_mx[2 * p + j]])
                last = bi == len(FF_BLOCKS) - 1
                for cb in range(4):
                    s = wload([(0, 512, wdown_v[l][:, tb:tb + nb, 512 * cb:512 * cb + 512])], kcn=nb)
                    for ft in range(4):
                        f = 4 * cb + ft
                        bk = big_mm(s, 128 * ft, mx, b_mx, nb, groups)
                        for gi, (t0, tn) in enumerate(groups):
                            tt(hT[:, f, t0:t0 + tn], banks[bk[gi]][:, 0:tn], hT[:, f, t0:t0 + tn], ALU.add,
                               [b_bank[bk[gi]]], [b_hT[f][gi]])
                        if last and feed_norm:
                            norm_feed(f, groups, Tc, delay=True)

        vt = sb("vtok", [128, 9, 2, DV + 2], BF16)
        try:
            for hf in range(n_halves):
                groups = [(0, 512), (512, 512)] + ([(NX, NM)] if hf == 0 else [])
                tiles = ([(NX, NM)] if hf == 0 else []) + [(128 * i, 128) for i in range(8)]
                Tc = T if hf == 0 else NX
                cur["hf"] = hf
                load_x(hf, groups, tiles, Tc)
                dump(f"x{hf}", hT[:, :, :], [b for k in range(KC) for b in b_hT[k]])
                chk("load")
                for k in range(KC):
                    norm_feed(k, groups, Tc)
                for l in range(n_layers):
                    norm_finish(C_NMW + l * 16, groups, Tc)
                    dump(f"hb{hf}{l}", hb[:, :, :], b_hb)
                    chk("norm")
                    gates_phase(l, hf, tiles)
                    chk("gates")
                    heads_phase(l, hf, groups, tiles, Tc)
                    dump(f"mx{hf}{l}", mx[:, :, :], b_mx)
                    chk("heads")
                    outproj_phase(l, groups, Tc)
                    dump(f"hm{hf}{l}", hT[:, :, :], [b for k in range(KC) for b in b_hT[k]])
                    norm_finish(C_NFW + l * 16, groups, Tc)
                    chk("outproj")
                    ffn_phase(l, groups, Tc, feed_norm=True)
                    dump(f"hf{hf}{l}", hT[:, :, :], [b for k in range(KC) for b in b_hT[k]])
                    chk("ffn")
                norm_finish(C_NFIN, groups, Tc, out_is_hb=False, Tout=NX)
                store_out(hf, groups)
        except _Stop:
            flush_norm(groups)
            if stop_after.split("@")[0] in ("conv", "gates"):
                dump(f"mx{hf}{l}", mx[:, :, :], b_mx)
        b_fin = [Buf("fin_act"), Buf("fin_dve")]
        act(ep[0:1, 6:7], one_col[0:1, 0:1], AF.Copy, [b_eps], [b_fin[0]])
        memset(ep[0:1, 7:8], 0.0, [b_fin[1]])
        every = [b for b in b_wb + b_bank + [b_wg]]
        P.op("sync", lambda e: None, b_fin + every, stage_bufs(0) + stage_bufs(1))

        sems = {}
        for e in ENGS:
            sems[("eng", e)] = es.enter_context(nc.semaphore(f"s_{e}"))
        for k in P.dma_keys():
            sems[("dma", k)] = es.enter_context(nc.semaphore(f"d_{k}"))
        bodies = P.emit(sems)
        with nc.Block() as block:
            block.sync(bodies["sync"])
            block.scalar(bodies["scalar"])
            block.vector(bodies["vector"])
            block.gpsimd(bodies["gpsimd"])
            block.tensor(bodies["tensor"])
    return nc


def host_tables(norm_mix_w, norm_ffn_w, norm_final_w, mlstm_norm_w, conv_w, b_gates):
    cols = np.zeros((128, NCOLS), np.float32)
    for l in range(2):
        cols[:, C_NMW + 16 * l:C_NMW + 16 * l + 16] = np.asarray(norm_mix_w[l], np.float32).reshape(16, 128).T
        cols[:, C_NFW + 16 * l:C_NFW + 16 * l + 16] = np.asarray(norm_ffn_w[l], np.float32).reshape(16, 128).T
        cols[:, C_MNW + 8 * l:C_MNW + 8 * l + 8] = np.asarray(mlstm_norm_w[l], np.float32).reshape(8, 128).T
        for j in range(3):
            cols[:, C_CVW + 24 * l + 8 * j:C_CVW + 24 * l + 8 * j + 8] = np.asarray(conv_w[l, j], np.float32).reshape(8, 128).T
        cols[:, C_BG + 8 * l:C_BG + 8 * l + 8] = np.broadcast_to(np.asarray(b_gates[l], np.float32)[None, :], (128, 8))
    cols[:, C_NFIN:C_NFIN + 16] = np.asarray(norm_final_w, np.float32).reshape(16, 128).T
    cst = np.zeros((128, 512), np.float32)
    s = np.arange(128)[:, None]
    t = np.arange(128)[None, :]
    cst[:, 0:128] = np.eye(128, dtype=np.float32)
    cst[:, 128:256] = (s <= t).astype(np.float32)
    cst[:, 256:384] = np.where(s <= t, 0.0, -30000.0).astype(np.float32)
    cst[:, 384:512] = 1.0
    return cols, cst


_NC_CACHE = {}


def kernel(x, meta_tokens, norm_mix_w, w_in, b_gates, conv_w, mlstm_norm_w, w_out,
           norm_ffn_w, w_gate, w_up, w_down, norm_final_w):
    x = np.asarray(x, np.float32)
    cols, cst = host_tables(np.asarray(norm_mix_w), np.asarray(norm_ffn_w), np.asarray(norm_final_w),
                            np.asarray(mlstm_norm_w), np.asarray(conv_w), np.asarray(b_gates))
    if "nc" not in _NC_CACHE:
        _NC_CACHE["nc"] = build_nc()
    nc = _NC_CACHE["nc"]
    shared = {
        "meta": np.ascontiguousarray(np.asarray(meta_tokens, np.float32)),
        "w_in": np.ascontiguousarray(np.asarray(w_in, np.float32)),
        "w_out": np.ascontiguousarray(np.asarray(w_out, np.float32)),
        "w_gate": np.ascontiguousarray(np.asarray(w_gate, np.float32)),
        "w_up": np.ascontiguousarray(np.asarray(w_up, np.float32)),
        "w_down": np.ascontiguousarray(np.asarray(w_down, np.float32)),
        "cols": cols, "cst": cst,
    }
    in_maps = [dict(shared, x=np.ascontiguousarray(x[b])) for b in range(8)]
    res = run_bass_kernel_spmd(nc, in_maps, core_ids=list(range(8)))
    return np.stack([np.asarray(r["out"], np.float32) for r in res.results], axis=0)
```

```python
from contextlib import ExitStack
import numpy as np
import concourse.bass as bass
import concourse.mybir as mybir
from concourse.bass_utils import run_bass_kernel_spmd

F32 = mybir.dt.float32
BF16 = mybir.dt.bfloat16
AF = mybir.ActivationFunctionType
ALU = mybir.AluOpType

D = 2048
KC = 16
NH = 4
DQK = 128
DV = 256
DFF = 5632
DIN = 6152
NX = 1024
NM = 16
T = NX + NM
EPS = 1e-6
GATE_CAP = 15.0
ENGS = ("sync", "scalar", "vector", "gpsimd", "tensor")

C_NMW, C_NFW, C_NFIN, C_MNW, C_CVW, C_BG, NCOLS = 0, 32, 64, 80, 96, 144, 160
FF_BLOCKS = [(0, 12), (12, 12), (24, 12), (36, 8)]


class Buf:
    __slots__ = ("name", "w", "rs", "excl")

    def __init__(self, name, excl=False):
        self.name = name
        self.w = None
        self.rs = []
        self.excl = excl


class Op:
    __slots__ = ("eng", "fn", "deps", "dma_key", "signal", "count", "group")

    def __init__(self, eng, fn, dma_key, group):
        self.eng = eng
        self.fn = fn
        self.deps = []
        self.dma_key = dma_key
        self.signal = dma_key is not None
        self.count = None
        self.group = group


class Prog:
    def __init__(self):
        self.ops = {e: [] for e in ENGS}

    def op(self, eng, fn, reads=(), writes=(), dma_key=None, group=None):
        o = Op(eng, fn, dma_key, group)
        deps = {}

        def add(p):
            if p is None:
                return
            if p.eng == "tensor" and eng == "tensor":
                return
            if group is not None and p.group == group:
                return
            deps[id(p)] = p

        for r in reads:
            add(r.w)
            if r.excl:
                for rd in r.rs:
                    if rd.eng != eng:
                        add(rd)
        for w in writes:
            add(w.w)
            for rd in w.rs:
                if rd.eng == eng and rd.dma_key is None and dma_key is None:
                    continue
                add(rd)
        for w in writes:
            w.w = o
            w.rs = []
        for r in reads:
            if dma_key is None:
                r.rs = [rd for rd in r.rs if rd.eng != eng or rd.dma_key is not None]
            r.rs.append(o)
        o.deps = list(deps.values())
        for p in o.deps:
            p.signal = True
        self.ops[eng].append(o)
        return o

    def dma_keys(self):
        ks = set()
        for e in ENGS:
            for o in self.ops[e]:
                if o.dma_key is not None:
                    ks.add(o.dma_key)
        return sorted(ks)

    def emit(self, sems):
        cnt = {}
        for e in ENGS:
            for o in self.ops[e]:
                if not o.signal:
                    continue
                key = ("dma", o.dma_key) if o.dma_key is not None else ("eng", e)
                cnt[key] = cnt.get(key, 0) + (16 if o.dma_key is not None else 1)
                o.count = (key, cnt[key])

        def run(e):
            def body(engobj):
                seen = {}
                for o in self.ops[e]:
                    need = {}
                    for p in o.deps:
                        k, v = p.count
                        if seen.get(k, 0) >= v:
                            continue
                        if need.get(k, 0) < v:
                            need[k] = v
                    for k, v in need.items():
                        engobj.wait_ge(sems[k], v)
                        seen[k] = v
                    ins = o.fn(engobj)
                    if o.signal:
                        k, v = o.count
                        ins.then_inc(sems[k], 16 if k[0] == "dma" else 1)
            return body
        return {e: run(e) for e in ENGS}


def build_nc(dbg=None, n_halves=2, n_layers=2, stop_after=None):
    dbg = dbg or []
    nc = bass.Bass("TRN2", target_bir_lowering=False)
    x_d = nc.dram_tensor("x", [2 * NX, D], F32, kind="ExternalInput").ap()
    meta_d = nc.dram_tensor("meta", [NM, D], F32, kind="ExternalInput").ap()
    w_in_d = nc.dram_tensor("w_in", [2, D, DIN], F32, kind="ExternalInput").ap()
    w_out_d = nc.dram_tensor("w_out", [2, D, D], F32, kind="ExternalInput").ap()
    w_gate_d = nc.dram_tensor("w_gate", [2, D, DFF], F32, kind="ExternalInput").ap()
    w_up_d = nc.dram_tensor("w_up", [2, D, DFF], F32, kind="ExternalInput").ap()
    w_down_d = nc.dram_tensor("w_down", [2, DFF, D], F32, kind="ExternalInput").ap()
    cols_d = nc.dram_tensor("cols", [128, NCOLS], F32, kind="ExternalInput").ap()
    cst_d = nc.dram_tensor("cst", [128, 512], F32, kind="ExternalInput").ap()
    out_d = nc.dram_tensor("out", [2 * NX, D], F32, kind="ExternalOutput").ap()
    dbg_d = {}
    for name in dbg:
        dbg_d[name] = nc.dram_tensor("dbg_" + name, [128, KC * T], F32, kind="ExternalOutput").ap()

    win_v = [w_in_d[l].rearrange("(kc p) n -> p kc n", p=128) for l in range(2)]
    wout_v = [w_out_d[l].rearrange("(kc p) n -> p kc n", p=128) for l in range(2)]
    wgate_v = [w_gate_d[l].rearrange("(kc p) n -> p kc n", p=128) for l in range(2)]
    wup_v = [w_up_d[l].rearrange("(kc p) n -> p kc n", p=128) for l in range(2)]
    wdown_v = [w_down_d[l].rearrange("(kc p) n -> p kc n", p=128) for l in range(2)]

    P = Prog()
    es = ExitStack()
    with es:
        sb = lambda name, shape, dt: es.enter_context(nc.sbuf_tensor(name, shape, dt))
        hT = sb("hT", [128, KC, T], F32)
        hb = sb("hb", [128, KC, T], BF16)
        mx = sb("mx", [128, KC, T], BF16)
        wb = [sb(f"wb{i}", [128, KC, 512], BF16) for i in range(2)]
        SW = 1044
        scr = [sb(f"scr{i}", [128, SW], F32) for i in range(4)]
        cst = sb("cst_sb", [128, 512], F32)
        cols = sb("cols_sb", [128, NCOLS], F32)
        ident_bf = sb("ident_bf", [128, 128], BF16)
        ones_bf = sb("ones_bf", [128, 128], BF16)
        eps_col = sb("eps_col", [128, 1], F32)
        one_col = sb("one_col", [128, 1], F32)
        wg = sb("wg", [128, KC, 8], BF16)
        graw = sb("graw", [128, 9, 8], F32)
        gth = sb("gth", [128, 9, 8], F32)
        gli = sb("gli", [128, 9, 4], F32)
        glf = sb("glf", [128, 9, 4], F32)
        gcol = sb("gcol", [128, 9, 4], F32)
        bcol = sb("bcol", [128, 9, 4], F32)
        gres = sb("gres", [128, 9, 4], F32)
        glf3 = sb("glf3", [128, 3, 9, 4], BF16)
        U_bf = sb("U_bf", [128, 128], BF16)
        cstate = sb("cstate", [128, 2 * NH, DV + 1], F32)
        cbf = sb("cbf", [128, DV + 2], BF16)
        ctail = sb("ctail", [128, 2 * 8, 2], F32)
        lfrep = sb("lfrep", [128, 128], F32)
        ebc = sb("ebc", [128, 128], F32)
        tmpm = sb("tmpm", [128, 128], F32)
        atm = sb("atm", [128, 128], F32)
        swt = sb("swt", [128, 128], BF16)
        qtil = sb("qtil", [128, 128], BF16)
        ktil = sb("ktil", [128, 128], BF16)
        hmn = sb("hmn", [128, DV], BF16)
        junk = sb("junk", [128, DV], BF16)
        ep = sb("ep", [128, 8], F32)
        banks = [es.enter_context(nc.psum_tensor(f"bank{i}", [128, 512], F32)) for i in range(8)]

        rstd = scr[3]
        ident_f = cst[:, 0:128]
        U_f = cst[:, 128:256]
        maskneg = cst[:, 256:384]
        ones_f = cst[:, 384:512]

        b_hT = [[Buf(f"hT{k}_{g}") for g in range(3)] for k in range(KC)]
        b_hb = [Buf(f"hb{k}") for k in range(KC)]
        b_mx = [Buf(f"mx{k}") for k in range(KC)]
        b_wb = [Buf("wb0"), Buf("wb1")]
        b_scr = [Buf(f"scr{i}") for i in range(4)]
        b_bank = [Buf(f"bank{i}", excl=True) for i in range(8)]
        b_cst, b_cols, b_identbf, b_onesbf, b_eps = Buf("cst"), Buf("cols"), Buf("identbf"), Buf("onesbf"), Buf("eps")
        b_wg, b_graw, b_gth, b_gli, b_glf, b_gcol = (Buf(n) for n in ("wg", "graw", "gth", "gli", "glf", "gcol"))
        b_rstd = b_scr[3]
        b_cstate = [Buf(f"cst{i}") for i in range(2 * NH)]
        b_cbf, b_ctail = Buf("cbf"), [Buf(f"ctail{i}") for i in range(16)]
        b_lfrep, b_ebc, b_tmpm, b_atm, b_swt, b_qtil, b_ktil, b_hmn, b_junk, b_ep = (
            Buf(n) for n in ("lfrep", "ebc", "tmpm", "atm", "swt", "qtil", "ktil", "hmn", "junk", "ep"))
        b_vt = Buf("vtok")
        b_bcol, b_gres, b_glf3 = Buf("bcol"), Buf("gres"), Buf("glf3")
        b_m0a, b_m0b, b_m0c, b_m0d = Buf("m0a"), Buf("m0b"), Buf("m0c"), [Buf("m0d0"), Buf("m0d1")]

        def hT_bufs(k, groups):
            return [b_hT[k][gi] for gi in range(len(groups))]

        def grp_of(t0):
            return 2 if t0 >= NX else t0 // 512

        def mm(out, lhsT, rhs, start, stop, reads, writes):
            P.op("tensor", lambda e: e.matmul(out, lhsT=lhsT, rhs=rhs, start=start, stop=stop), reads, writes)

        def tr(out, in_, ident, reads, writes):
            P.op("tensor", lambda e: e.transpose(out, in_, ident), reads, writes)

        def act(out, in_, func, reads, writes, bias=None, scale=None, accum_out=None):
            kw = {}
            if bias is not None:
                kw["bias"] = bias
            if scale is not None:
                kw["scale"] = scale
            if accum_out is not None:
                kw["accum_out"] = accum_out
            P.op("scalar", lambda e: e.activation(out, in_, func, **kw), reads, writes)

        def tt(out, in0, in1, op, reads, writes, eng="vector"):
            P.op(eng, lambda e: e.tensor_tensor(out, in0, in1, op), reads, writes)

        def stt(out, in0, scalar, in1, op0, op1, reads, writes):
            P.op("vector", lambda e: e.scalar_tensor_tensor(out, in0, scalar, in1, op0, op1), reads, writes)

        def ts(out, in0, s1, s2, op0, op1, reads, writes, eng="vector"):
            if s2 is None:
                P.op(eng, lambda e: e.tensor_scalar(out, in0, s1, None, op0), reads, writes)
            else:
                P.op(eng, lambda e: e.tensor_scalar(out, in0, s1, s2, op0, op1), reads, writes)

        def cp(out, in_, reads, writes, eng="vector"):
            if eng == "scalar":
                act(out, in_, AF.Copy, reads, writes)
            else:
                P.op(eng, lambda e: e.tensor_copy(out, in_), reads, writes)

        def memset(ap, val, writes, eng="vector"):
            P.op(eng, lambda e: e.memset(ap, val), (), writes)

        def dma(eng, out, in_, reads, writes, key, group=None):
            P.op(eng, lambda e: e.dma_start(out=out, in_=in_), reads, writes, dma_key=key, group=group)

        class _Stop(Exception):
            pass

        cur = {"hf": 0}

        def chk(stage):
            if stop_after == stage or stop_after == f"{stage}@{cur['hf']}":
                raise _Stop()

        dump_i = [0]

        def dump(name, src_ap, reads):
            if name in dbg_d:
                tcur = T if cur["hf"] == 0 else NX
                dst = dbg_d[name][:, :].rearrange("p (a b) -> p a b", a=KC)[:, :, 0:tcur]
                dump_i[0] += 1
                dma("gpsimd", dst, src_ap[:, :, 0:tcur], reads, (), f"dbg{dump_i[0]}")

        wstate = {"i": 0, "g": 0}

        def wload(pieces, kcn=KC):
            s = wstate["i"] % 2
            wstate["i"] += 1
            wstate["g"] += 1
            for (off, ncol, src) in pieces:
                dma("gpsimd", wb[s][:, 0:kcn, off:off + ncol], src, (), [b_wb[s]], f"wb{s}", group=("w", wstate["g"]))
            return s

        ring = {"i": 0, "set": [0, 1, 2, 3, 4]}

        def next_bank():
            st = ring["set"]
            b = st[ring["i"] % len(st)]
            ring["i"] += 1
            return b

        def big_mm(slot, coloff, rhs, rhs_bufs, kcn, groups):
            bks = [next_bank() for _ in groups]
            for k in range(kcn):
                for gi, (t0, tn) in enumerate(groups):
                    mm(banks[bks[gi]][:, 0:tn], wb[slot][:, k, coloff:coloff + 128], rhs[:, k, t0:t0 + tn],
                       k == 0, k == kcn - 1, [b_wb[slot], rhs_bufs[k]], [b_bank[bks[gi]]])
            return bks

        ev = {"i": 0}

        def evac_engine():
            ev["i"] += 1
            return "scalar" if ev["i"] % 2 else "vector"

        dma("sync", cst[:, :], cst_d[:, :], (), [b_cst], "c_cst")
        dma("sync", cols[:, :], cols_d[:, :], (), [b_cols], "c_cols")
        cp(ident_bf[:, :], ident_f, [b_cst], [b_identbf])
        cp(ones_bf[:, :], ones_f, [b_cst], [b_onesbf])
        cp(U_bf[:, :], U_f, [b_cst], [b_identbf])
        memset(bcol[:, :, :], 0.0, [b_bcol])
        memset(eps_col[:, :], EPS, [b_eps])
        memset(one_col[:, :], 1.0, [b_eps])
        memset(graw[:, :, :], 0.0, [b_graw])

        nstate = {"pend": None}

        def norm_feed_act(k, groups, Tc):
            sq = scr[k % 2][:, 0:T // 2 + 4].bitcast(BF16)
            act(sq[:, 0:Tc], hT[:, k, 0:Tc], AF.Square, hT_bufs(k, groups), [b_scr[k % 2]])
            return sq

        def norm_feed_mm(k, sq, groups):
            for gi, (t0, tn) in enumerate(groups):
                mm(banks[5 + gi][:, 0:tn], ones_bf[:, :], sq[:, t0:t0 + tn], k == 0, k == KC - 1,
                   [b_onesbf, b_scr[k % 2]], [b_bank[5 + gi]])

        def norm_feed(k, groups, Tc, delay=False):
            sq = norm_feed_act(k, groups, Tc)
            if delay:
                flush_norm(groups)
                nstate["pend"] = (k, sq)
            else:
                norm_feed_mm(k, sq, groups)

        def flush_norm(groups):
            if nstate["pend"] is not None:
                k, sq = nstate["pend"]
                nstate["pend"] = None
                norm_feed_mm(k, sq, groups)

        def norm_finish(wcol0, groups, Tc, out_is_hb=True, Tout=None):
            flush_norm(groups)
            for gi, (t0, tn) in enumerate(groups):
                act(rstd[:, t0:t0 + tn], banks[5 + gi][:, 0:tn], AF.Ln, [b_bank[5 + gi], b_eps], [b_rstd],
                    bias=eps_col[:, 0:1], scale=1.0 / D)
            act(rstd[:, 0:Tc], rstd[:, 0:Tc], AF.Exp, [b_rstd], [b_rstd], scale=-0.5)
            Tout = Tc if Tout is None else Tout
            for k in range(KC):
                if out_is_hb:
                    stt(hb[:, k, 0:Tout], hT[:, k, 0:Tout], cols[:, wcol0 + k:wcol0 + k + 1], rstd[:, 0:Tout],
                        ALU.mult, ALU.mult, hT_bufs(k, groups) + [b_cols, b_rstd], [b_hb[k]])
                else:
                    stt(hT[:, k, 0:Tout], hT[:, k, 0:Tout], cols[:, wcol0 + k:wcol0 + k + 1], rstd[:, 0:Tout],
                        ALU.mult, ALU.mult, [b_cols, b_rstd], hT_bufs(k, groups))

        def stage_view(i):
            return mx[:, 4 * i:4 * i + 4, :].rearrange("p a b -> p (a b)").bitcast(F32)

        def stage_bufs(i):
            return [b_mx[4 * i + j] for j in range(4)]

        def load_x(hf, groups, tiles, Tc):
            si = 0
            for (t0, n) in tiles:
                st = stage_view(si % 2)
                sbuf = stage_bufs(si % 2)
                if t0 >= NX:
                    dma("sync", st[0:n, 0:D], meta_d[:, :], (), sbuf, f"st{si % 2}")
                    bk = next_bank()
                    for k in range(KC):
                        tr(banks[bk][:, n * k:n * k + n], st[0:n, 128 * k:128 * k + 128], ident_f[0:n, 0:n],
                           sbuf + [b_cst], [b_bank[bk]])
                    cp(hT[:, :, t0:t0 + n], banks[bk][:, 0:n * KC].rearrange("p (a b) -> p a b", a=KC),
                       [b_bank[bk]], [b_hT[k][2] for k in range(KC)], eng=evac_engine())
                else:
                    r0 = hf * NX + t0
                    dma("sync", st[:, 0:D], x_d[r0:r0 + 128, :], (), sbuf, f"st{si % 2}")
                    g = grp_of(t0)
                    for kq in range(4):
                        bk = next_bank()
                        for j in range(4):
                            k = 4 * kq + j
                            tr(banks[bk][:, 128 * j:128 * j + 128], st[:, 128 * k:128 * k + 128], ident_f,
                               sbuf + [b_cst], [b_bank[bk]])
                        cp(hT[:, 4 * kq:4 * kq + 4, t0:t0 + 128],
                           banks[bk][:, :].rearrange("p (a b) -> p a b", a=4),
                           [b_bank[bk]], [b_hT[4 * kq + j][g] for j in range(4)], eng=evac_engine())
                si += 1

        def store_out(hf, groups):
            si = 0
            for i in range(8):
                t0 = 128 * i
                g = grp_of(t0)
                st = stage_view(si % 2)
                sbuf = stage_bufs(si % 2)
                for kq in range(4):
                    bk = next_bank()
                    for j in range(4):
                        k = 4 * kq + j
                        tr(banks[bk][:, 128 * j:128 * j + 128], hT[:, k, t0:t0 + 128], ident_f,
                           [b_hT[k][g], b_cst], [b_bank[bk]])
                    cp(st[:, 512 * kq:512 * kq + 512], banks[bk][:, :], [b_bank[bk]], sbuf, eng=evac_engine())
                r0 = hf * NX + t0
                dma("sync", out_d[r0:r0 + 128, :], st[:, 0:D], sbuf, (), f"so{si % 2}")
                si += 1

        def conv_gen(l, hf, groups, Tc):
            abuf, cbuf = scr[2], scr[3]
            b_ab, b_cb = b_scr[2], b_scr[3]
            xoff = 2 + (NM if hf == 0 else 0)

            def aoff(t0):
                return 2 if t0 >= NX else xoff + t0

            for c in range(8):
                s = wload([(0, 128, win_v[l][:, :, 3080 + 128 * c:3080 + 128 * c + 128]),
                           (128, 128, win_v[l][:, :, 4104 + 128 * c:4104 + 128 * c + 128]),
                           (256, 128, win_v[l][:, :, 5128 + 128 * c:5128 + 128 * c + 128])])
                bk = big_mm(s, 0, hb, b_hb, KC, groups)
                for gi, (t0, tn) in enumerate(groups):
                    act(abuf[:, aoff(t0):aoff(t0) + tn], banks[bk[gi]][:, 0:tn], AF.Copy, [b_bank[bk[gi]]], [b_ab])
                if hf == 0:
                    memset(abuf[:, 0:2], 0.0, [b_ab])
                else:
                    cp(abuf[:, 0:2], ctail[:, l * 8 + c, :], [b_ctail[l * 8 + c]], [b_ab])
                yield 1
                bk = big_mm(s, 256, hb, b_hb, KC, groups)
                for gi, (t0, tn) in enumerate(groups):
                    tt(abuf[:, aoff(t0):aoff(t0) + tn], banks[bk[gi]][:, 0:tn], abuf[:, aoff(t0):aoff(t0) + tn], ALU.mult,
                       [b_bank[bk[gi]]], [b_ab])
                yield 1
                bk = big_mm(s, 128, hb, b_hb, KC, groups)
                w0 = cols[:, C_CVW + l * 24 + 0 * 8 + c:C_CVW + l * 24 + 0 * 8 + c + 1]
                w1 = cols[:, C_CVW + l * 24 + 1 * 8 + c:C_CVW + l * 24 + 1 * 8 + c + 1]
                w2 = cols[:, C_CVW + l * 24 + 2 * 8 + c:C_CVW + l * 24 + 2 * 8 + c + 1]
                ts(cbuf[:, 0:Tc], abuf[:, 2:2 + Tc], w2, None, ALU.mult, None, [b_ab, b_cols], [b_cb])
                stt(cbuf[:, 0:Tc], abuf[:, 1:1 + Tc], w1, cbuf[:, 0:Tc], ALU.mult, ALU.add, [b_ab, b_cols], [b_cb])
                stt(cbuf[:, 0:Tc], abuf[:, 0:Tc], w0, cbuf[:, 0:Tc], ALU.mult, ALU.add, [b_ab, b_cols], [b_cb])
                cp(ctail[:, l * 8 + c, :], abuf[:, Tc:Tc + 2], [b_ab], [b_ctail[l * 8 + c]], eng="scalar")
                for gi, (t0, tn) in enumerate(groups):
                    co = aoff(t0) - 2
                    tt(mx[:, 8 + c, t0:t0 + tn], banks[bk[gi]][:, 0:tn], cbuf[:, co:co + tn], ALU.mult,
                       [b_bank[bk[gi]], b_cb], [b_mx[8 + c]])
                yield 1

        def gates_phase(l, hf, tiles):
            dma("gpsimd", wg[:, :, :], win_v[l][:, :, 3072:3080], (), [b_wg], "wg")
            gbank = 5
            for (t0, n) in tiles:
                sl = t0 // 128
                for k in range(KC):
                    mm(banks[gbank][0:n, 8 * sl:8 * sl + 8], hb[:, k, t0:t0 + n], wg[:, k, :], k == 0, k == KC - 1,
                       [b_hb[k], b_wg], [b_bank[gbank]])
            for (t0, n) in tiles:
                sl = t0 // 128
                tt(graw[0:n, sl, :], banks[gbank][0:n, 8 * sl:8 * sl + 8], cols[0:n, C_BG + 8 * l:C_BG + 8 * l + 8],
                   ALU.add, [b_bank[gbank], b_cols], [b_graw])
            act(gth[:, :, :], graw[:, :, :], AF.Tanh, [b_graw], [b_gth], scale=1.0 / GATE_CAP)
            ts(gli[:, :, :], gth[:, :, 0:4], GATE_CAP, None, ALU.mult, None, [b_gth], [b_gli])
            act(glf[:, :, :], gth[:, :, 4:8], AF.Exp, [b_gth], [b_glf], scale=-GATE_CAP)
            act(glf[:, :, :], glf[:, :, :], AF.Ln, [b_glf, b_eps], [b_glf], bias=one_col[:, 0:1])
            ts(glf[:, :, :], glf[:, :, :], -1.0, None, ALU.mult, None, [b_glf], [b_glf])
            ts(glf3[:, 0, :, :], glf[:, :, :], 1.0, None, ALU.mult, None, [b_glf], [b_glf3])
            tt(gres[:, :, :], glf[:, :, :], glf3[:, 0, :, :], ALU.subtract, [b_glf, b_glf3], [b_gres])
            ts(glf3[:, 1, :, :], gres[:, :, :], 1.0, None, ALU.mult, None, [b_gres], [b_glf3])
            tt(gres[:, :, :], gres[:, :, :], glf3[:, 1, :, :], ALU.subtract, [b_glf3], [b_gres])
            ts(glf3[:, 2, :, :], gres[:, :, :], 1.0, None, ALU.mult, None, [b_gres], [b_glf3])
            bbank = 6
            for (t0, n) in tiles:
                sl = t0 // 128
                for i3 in range(3):
                    mm(banks[bbank][0:n, 4 * sl:4 * sl + 4], U_bf[0:n, 0:n], glf3[0:n, i3, sl, :], i3 == 0, i3 == 2,
                       [b_identbf, b_glf3], [b_bank[bbank]])
            for (t0, n) in tiles:
                sl = t0 // 128
                cp(bcol[0:n, sl, :], banks[bbank][0:n, 4 * sl:4 * sl + 4], [b_bank[bbank]], [b_bcol])
            for (t0, n) in tiles:
                sl = t0 // 128
                tt(gcol[0:n, sl, :], gli[0:n, sl, :], bcol[0:n, sl, :], ALU.subtract, [b_gli, b_bcol], [b_gcol])

        def mlstm_slots(l, hf, h, hh, tiles, qT, kT, ogs, vtok):
            sidx = l * NH + h
            cst_h = cstate[:, sidx, :]
            nw0 = C_MNW + l * 8 + 2 * h
            MBA, MBN, MBC = 5, 6, 7
            bA, bN, bC = b_bank[MBA], b_bank[MBN], b_bank[MBC]
            Bbc_r = banks[MBA][:, 0:128]
            kps_r = banks[MBA][:, 128:192].bitcast(BF16)
            ST_r = banks[MBA][:, 256:384]
            hmT_r = [banks[MBA][:, 384 + 64 * j:448 + 64 * j].bitcast(BF16) for j in range(2)]
            num_r = banks[MBN]
            dC_r = banks[MBC]
            nt = len(tiles)
            if hf > 0:
                cp(cbf[:, 0:DV + 1], cst_h, [b_cstate[sidx]], [b_cbf], eng="scalar")

            def lfrep_op(c):
                t0, n = tiles[c]
                ts(lfrep[0:n, :], ones_f[0:n, :], bcol[0:n, t0 // 128, h:h + 1], None, ALU.mult, None, [b_cst, b_bcol], [b_lfrep])

            lfrep_op(0)
            for s in range(nt + 2):
                cC, cB, cA = s - 2, s - 1, s
                hasC = 0 <= cC < nt
                hasB = 0 <= cB < nt
                hasA = cA < nt
                if hasC:
                    t0, n = tiles[cC]
                    for j in range(2):
                        tr(hmT_r[j][:, 0:n], hmn[0:n, 128 * j:128 * j + 128], ident_bf[0:n, 0:n], [b_hmn, b_identbf], [bA])
                if hasB:
                    t0, n = tiles[cB]
                    sl = t0 // 128
                    firstB = (hf == 0 and cB == 0)
                    if not firstB:
                        mm(num_r[0:n, 0:DV + 1], qtil[:, 0:n], cbf[:, 0:DV + 1], True, False, [b_qtil, b_cbf], [bN])
                    mm(num_r[0:n, 0:DV + 1], swt[0:n, 0:n], vtok[0:n, sl, hh, 0:DV + 1], firstB, True, [b_swt, b_vt], [bN])
                    mm(dC_r[:, 0:DV + 1], ktil[0:n, :], vtok[0:n, sl, hh, 0:DV + 1], True, True, [b_ktil, b_vt], [bC])
                if hasA:
                    t0, n = tiles[cA]
                    tr(Bbc_r[:, 0:n], lfrep[0:n, :], ident_f[0:n, 0:n], [b_lfrep, b_cst], [bA])
                    mm(ST_r[0:n, 0:n], kT[:, t0:t0 + n], qT[:, t0:t0 + n], True, True, [b_scr[0]], [bA])
                    tr(kps_r[0:n, :], kT[:, t0:t0 + n], ident_bf[:, :], [b_scr[0], b_identbf], [bA])
                if hasB:
                    t0, n = tiles[cB]
                    if firstB:
                        cp(cst_h, dC_r[:, 0:DV + 1], [bC], [b_cstate[sidx]])
                    else:
                        stt(cst_h, cst_h, ebc[:, n - 1:n], dC_r[:, 0:DV + 1], ALU.mult, ALU.add, [b_ebc, bC], [b_cstate[sidx]])
                    cp(cbf[:, 0:DV + 1], cst_h, [b_cstate[sidx]], [b_cbf], eng="scalar")
                if hasA:
                    t0, n = tiles[cA]
                    sl = t0 // 128
                    firstA = (hf == 0 and cA == 0)
                    act(ebc[:, 0:n], Bbc_r[:, 0:n], AF.Exp, [bA], [b_ebc])
                    ts(tmpm[0:n, 0:n], Bbc_r[0:n, 0:n], gcol[0:n, sl, h:h + 1], None, ALU.add, None, [bA, b_gcol], [b_tmpm])
                    tt(tmpm[0:n, 0:n], tmpm[0:n, 0:n], maskneg[0:n, 0:n], ALU.add, [b_cst], [b_tmpm])
                if hasC:
                    t0, n = tiles[cC]
                    for j in range(2):
                        stt(mx[:, 2 * h + j, t0:t0 + n], hmT_r[j][:, 0:n], cols[:, nw0 + j:nw0 + j + 1], ogs[:, j, t0:t0 + n],
                            ALU.mult, ALU.mult, [bA, b_cols, b_scr[1]], [b_mx[2 * h + j]])
                if hasA:
                    t0, n = tiles[cA]
                    act(atm[0:n, 0:n], tmpm[0:n, 0:n], AF.Exp, [b_tmpm], [b_atm])
                    tt(swt[0:n, 0:n], ST_r[0:n, 0:n], atm[0:n, 0:n], ALU.mult, [bA, b_atm], [b_swt])
                    ts(ktil[0:n, :], kps_r[0:n, :], atm[0:n, n - 1:n], None, ALU.mult, None, [bA, b_atm], [b_ktil])
                    if not firstA:
                        tt(qtil[:, 0:n], qT[:, t0:t0 + n], ebc[:, 0:n], ALU.mult, [b_scr[0], b_ebc], [b_qtil])
                if hasB:
                    t0, n = tiles[cB]
                    act(junk[0:n, :], num_r[0:n, 0:DV], AF.Square, [bN], [b_junk, b_ep], accum_out=ep[0:n, 0:1])
                    cp(ep[0:n, 1:2], num_r[0:n, DV:DV + 1], [bN], [b_ep])
                    tt(ep[0:n, 2:3], ep[0:n, 1:2], ep[0:n, 1:2], ALU.mult, [b_ep], [b_ep])
                    ts(ep[0:n, 2:3], ep[0:n, 2:3], 1.0, EPS, ALU.max, ALU.mult, [b_ep], [b_ep])
                    stt(ep[0:n, 3:4], ep[0:n, 0:1], 1.0 / DV, ep[0:n, 2:3], ALU.mult, ALU.add, [b_ep], [b_ep])
                    act(ep[0:n, 4:5], ep[0:n, 3:4], AF.Ln, [b_ep], [b_ep])
                    act(ep[0:n, 5:6], ep[0:n, 4:5], AF.Exp, [b_ep], [b_ep], scale=-0.5)
                    ts(hmn[0:n, :], num_r[0:n, 0:DV], ep[0:n, 5:6], None, ALU.mult, None, [bN, b_ep], [b_hmn])
                if cA + 1 < nt:
                    lfrep_op(cA + 1)
                yield

        def heads_phase(l, hf, groups, tiles, Tc):
            conv = conv_gen(l, hf, groups, Tc)
            cstate_ = {"mid": 0}
            qk = scr[0][:, 0:T].bitcast(BF16)
            qT = qk[:, 0:T]
            kT = qk[:, T:2 * T]
            ogs = scr[1][:, 0:T].bitcast(BF16).rearrange("p (a b) -> p a b", a=2)
            vtok = vt[:, :, :, :]
            for pair in range(2):
                s = wload([(0, 512, win_v[l][:, :, 1024 + 512 * pair:1024 + 512 * pair + 512])])
                for (t0, n) in tiles:
                    sl = t0 // 128
                    bk = next_bank()
                    for k in range(KC):
                        mm(banks[bk][0:n, :], hb[:, k, t0:t0 + n], wb[s][:, k, :], k == 0, k == KC - 1,
                           [b_hb[k], b_wb[s]], [b_bank[bk]])
                    cp(vtok[0:n, sl, :, 0:DV], banks[bk][0:n, :].rearrange("p (a b) -> p a b", a=2),
                       [b_bank[bk]], [b_vt], eng=evac_engine())
                chk("h_v")
                memset(vtok[:, :, :, DV:DV + 1], 1.0, [b_vt])
                chk("h_vm")
                for hh in range(2):
                    h = 2 * pair + hh
                    s = wload([(0, 128, win_v[l][:, :, 128 * h:128 * h + 128]),
                               (128, 128, win_v[l][:, :, 512 + 128 * h:512 + 128 * h + 128]),
                               (256, 256, win_v[l][:, :, 2048 + 256 * h:2048 + 256 * h + 256])])
                    bk = big_mm(s, 0, hb, b_hb, KC, groups)
                    for gi, (t0, tn) in enumerate(groups):
                        act(qT[:, t0:t0 + tn], banks[bk[gi]][:, 0:tn], AF.Copy, [b_bank[bk[gi]]], [b_scr[0]], scale=DQK ** -0.5)
                    bk = big_mm(s, 128, hb, b_hb, KC, groups)
                    for gi, (t0, tn) in enumerate(groups):
                        cp(kT[:, t0:t0 + tn], banks[bk[gi]][:, 0:tn], [b_bank[bk[gi]]], [b_scr[0]])
                    for j in range(2):
                        bk = big_mm(s, 256 + 128 * j, hb, b_hb, KC, groups)
                        for gi, (t0, tn) in enumerate(groups):
                            act(ogs[:, j, t0:t0 + tn], banks[bk[gi]][:, 0:tn], AF.Sigmoid, [b_bank[bk[gi]]], [b_scr[1]])
                    nslots = len(tiles) + 2
                    for si, _ in enumerate(mlstm_slots(l, hf, h, hh, tiles, qT, kT, ogs, vtok)):
                        if cstate_["mid"] > 0 or nslots - si >= 3:
                            if next(conv, None) is not None:
                                cstate_["mid"] = (cstate_["mid"] + 1) % 3

            for _ in conv:
                pass

        def outproj_phase(l, groups, Tc):
            for blk in range(4):
                s = wload([(0, 512, wout_v[l][:, :, 512 * blk:512 * blk + 512])])
                for ft in range(4):
                    f = 4 * blk + ft
                    bk = big_mm(s, 128 * ft, mx, b_mx, KC, groups)
                    for gi, (t0, tn) in enumerate(groups):
                        tt(hT[:, f, t0:t0 + tn], banks[bk[gi]][:, 0:tn], hT[:, f, t0:t0 + tn], ALU.add,
                           [b_bank[bk[gi]]], [b_hT[f][gi]])
                    norm_feed(f, groups, Tc, delay=True)

        def ffn_phase(l, groups, Tc, feed_norm):
            sil = [scr[0], scr[1]]
            for bi, (tb, nb) in enumerate(FF_BLOCKS):
                ring["set"] = [0, 1, 2, 3, 4, 5, 6, 7]
                for p in range(nb // 2):
                    i0 = tb + 2 * p
                    s = wload([(0, 256, wgate_v[l][:, :, 128 * i0:128 * i0 + 256]),
                               (256, 256, wup_v[l][:, :, 128 * i0:128 * i0 + 256])])
                    for j in range(2):
                        bk = big_mm(s, 128 * j, hb, b_hb, KC, groups)
                        for gi, (t0, tn) in enumerate(groups):
                            act(sil[j][:, t0:t0 + tn], banks[bk[gi]][:, 0:tn], AF.Silu, [b_bank[bk[gi]]], [b_scr[j]])
                        bk = big_mm(s, 256 + 128 * j, hb, b_hb, KC, groups)
                        for gi, (t0, tn) in enumerate(groups):
                            tt(mx[:, 2 * p + j, t0:t0 + tn], banks[bk[gi]][:, 0:tn], sil[j][:, t0:t0 + tn], ALU.mult,
                               [b_bank[bk[gi]], b_scr[j]], [b_mx[2 * p + j]])
                last = bi == len(FF_BLOCKS) - 1
                if last and feed_norm:
                    ring["set"] = [0, 1, 2, 3, 4]
                for cb in range(4):
                    s = wload([(0, 512, wdown_v[l][:, tb:tb + nb, 512 * cb:512 * cb + 512])], kcn=nb)
                    for ft in range(4):
                        f = 4 * cb + ft
                        bk = big_mm(s, 128 * ft, mx, b_mx, nb, groups)
                        for gi, (t0, tn) in enumerate(groups):
                            tt(hT[:, f, t0:t0 + tn], banks[bk[gi]][:, 0:tn], hT[:, f, t0:t0 + tn], ALU.add,
                               [b_bank[bk[gi]]], [b_hT[f][gi]])
                        if last and feed_norm:
                            norm_feed(f, groups, Tc, delay=True)
            ring["set"] = [0, 1, 2, 3, 4]

        vt = sb("vtok", [128, 9, 2, DV + 2], BF16)
        try:
            for hf in range(n_halves):
                groups = [(0, 512), (512, 512)] + ([(NX, NM)] if hf == 0 else [])
                tiles = ([(NX, NM)] if hf == 0 else []) + [(128 * i, 128) for i in range(8)]
                Tc = T if hf == 0 else NX
                cur["hf"] = hf
                load_x(hf, groups, tiles, Tc)
                dump(f"x{hf}", hT[:, :, :], [b for k in range(KC) for b in b_hT[k]])
                chk("load")
                for k in range(KC):
                    norm_feed(k, groups, Tc)
                for l in range(n_layers):
                    norm_finish(C_NMW + l * 16, groups, Tc)
                    dump(f"hb{hf}{l}", hb[:, :, :], b_hb)
                    chk("norm")
                    gates_phase(l, hf, tiles)
                    chk("gates")
                    heads_phase(l, hf, groups, tiles, Tc)
                    dump(f"mx{hf}{l}", mx[:, :, :], b_mx)
                    chk("heads")
                    outproj_phase(l, groups, Tc)
                    dump(f"hm{hf}{l}", hT[:, :, :], [b for k in range(KC) for b in b_hT[k]])
                    norm_finish(C_NFW + l * 16, groups, Tc)
                    chk("outproj")
                    ffn_phase(l, groups, Tc, feed_norm=True)
                    dump(f"hf{hf}{l}", hT[:, :, :], [b for k in range(KC) for b in b_hT[k]])
                    chk("ffn")
                norm_finish(C_NFIN, groups, Tc, out_is_hb=False, Tout=NX)
                store_out(hf, groups)
        except _Stop:
            flush_norm(groups)
            if stop_after.split("@")[0] in ("conv", "gates"):
                dump(f"mx{hf}{l}", mx[:, :, :], b_mx)
        b_fin = [Buf("fin_act"), Buf("fin_dve")]
        act(ep[0:1, 6:7], one_col[0:1, 0:1], AF.Copy, [b_eps], [b_fin[0]])
        memset(ep[0:1, 7:8], 0.0, [b_fin[1]])
        every = [b for b in b_wb + b_bank + [b_wg]]
        P.op("sync", lambda e: None, b_fin + every, stage_bufs(0) + stage_bufs(1))

        sems = {}
        for e in ENGS:
            sems[("eng", e)] = es.enter_context(nc.semaphore(f"s_{e}"))
        for k in P.dma_keys():
            sems[("dma", k)] = es.enter_context(nc.semaphore(f"d_{k}"))
        bodies = P.emit(sems)
        with nc.Block() as block:
            block.sync(bodies["sync"])
            block.scalar(bodies["scalar"])
            block.vector(bodies["vector"])
            block.gpsimd(bodies["gpsimd"])
            block.tensor(bodies["tensor"])
    return nc


def host_tables(norm_mix_w, norm_ffn_w, norm_final_w, mlstm_norm_w, conv_w, b_gates):
    cols = np.zeros((128, NCOLS), np.float32)
    for l in range(2):
        cols[:, C_NMW + 16 * l:C_NMW + 16 * l + 16] = np.asarray(norm_mix_w[l], np.float32).reshape(16, 128).T
        cols[:, C_NFW + 16 * l:C_NFW + 16 * l + 16] = np.asarray(norm_ffn_w[l], np.float32).reshape(16, 128).T
        cols[:, C_MNW + 8 * l:C_MNW + 8 * l + 8] = np.asarray(mlstm_norm_w[l], np.float32).reshape(8, 128).T
        for j in range(3):
            cols[:, C_CVW + 24 * l + 8 * j:C_CVW + 24 * l + 8 * j + 8] = np.asarray(conv_w[l, j], np.float32).reshape(8, 128).T
        cols[:, C_BG + 8 * l:C_BG + 8 * l + 8] = np.broadcast_to(np.asarray(b_gates[l], np.float32)[None, :], (128, 8))
    cols[:, C_NFIN:C_NFIN + 16] = np.asarray(norm_final_w, np.float32).reshape(16, 128).T
    cst = np.zeros((128, 512), np.float32)
    s = np.arange(128)[:, None]
    t = np.arange(128)[None, :]
    cst[:, 0:128] = np.eye(128, dtype=np.float32)
    cst[:, 128:256] = (s <= t).astype(np.float32)
    cst[:, 256:384] = np.where(s <= t, 0.0, -30000.0).astype(np.float32)
    cst[:, 384:512] = 1.0
    return cols, cst


_NC_CACHE = {}


def kernel(x, meta_tokens, norm_mix_w, w_in, b_gates, conv_w, mlstm_norm_w, w_out,
           norm_ffn_w, w_gate, w_up, w_down, norm_final_w):
    x = np.asarray(x, np.float32)
    cols, cst = host_tables(np.asarray(norm_mix_w), np.asarray(norm_ffn_w), np.asarray(norm_final_w),
                            np.asarray(mlstm_norm_w), np.asarray(conv_w), np.asarray(b_gates))
    if "nc" not in _NC_CACHE:
        _NC_CACHE["nc"] = build_nc()
    nc = _NC_CACHE["nc"]
    shared = {
        "meta": np.ascontiguousarray(np.asarray(meta_tokens, np.float32)),
        "w_in": np.ascontiguousarray(np.asarray(w_in, np.float32)),
        "w_out": np.ascontiguousarray(np.asarray(w_out, np.float32)),
        "w_gate": np.ascontiguousarray(np.asarray(w_gate, np.float32)),
        "w_up": np.ascontiguousarray(np.asarray(w_up, np.float32)),
        "w_down": np.ascontiguousarray(np.asarray(w_down, np.float32)),
        "cols": cols, "cst": cst,
    }
    in_maps = [dict(shared, x=np.ascontiguousarray(x[b])) for b in range(8)]
    res = run_bass_kernel_spmd(nc, in_maps, core_ids=list(range(8)))
    return np.stack([np.asarray(r["out"], np.float32) for r in res.results], axis=0)
```
